# Optimizing a Trainium2 kernel written in Bass

```python
import math
import jax, jax.numpy as jnp
from jax import lax
import numpy as np

D_MODEL = 1024
BATCH = 4
SEQ = 4096
DEPTH = 1

CHUNK = 64
SSD_EXPAND = 2
SSD_INNER = SSD_EXPAND * D_MODEL
SSD_HEADDIM = 64
SSD_HEADS = SSD_INNER // SSD_HEADDIM
SSD_GROUPS = 4
SSD_STATE = 128
SSD_CONV = 4
SSD_XBC = SSD_INNER + 2 * SSD_GROUPS * SSD_STATE
SSD_CHUNK = CHUNK
ATT_HEADS = 16
ATT_HEADDIM = 64
ATT_LATENT = 256
IDX_HEADS = 8
IDX_HEADDIM = 64
TOPK_MAX = 256
Q_BLOCK = 128
REL_BUCKETS = 32
REL_MAX_DIST = 128
D_FF = 2816
FFN_CONV = 3
DN_ALPHA = (2.0 * DEPTH) ** 0.25
DN_BETA = (8.0 * DEPTH) ** -0.25
LN_EPS = 1e-5

kernel_name = "hybrid_ssd_dsa_gated_deepnorm_block"


def _in_split_sizes():
    return (SSD_INNER, SSD_XBC, SSD_HEADS,
            ATT_HEADS * ATT_HEADDIM, ATT_LATENT, IDX_HEADS * IDX_HEADDIM, IDX_HEADDIM, IDX_HEADS,
            D_MODEL, D_MODEL)


def _split_points():
    pts, acc = [], 0
    for s in _in_split_sizes()[:-1]:
        acc += s
        pts.append(acc)
    return pts


def _layernorm(x, g, b):
    xf = x.astype(jnp.float32)
    mu = jnp.mean(xf, -1, keepdims=True)
    var = jnp.mean(jnp.square(xf - mu), -1, keepdims=True)
    return ((xf - mu) * lax.rsqrt(var + LN_EPS) * g + b).astype(x.dtype)


def _rmsnorm(x, g):
    xf = x.astype(jnp.float32)
    return (xf * lax.rsqrt(jnp.mean(xf * xf, -1, keepdims=True) + LN_EPS) * g).astype(x.dtype)


def _causal_dwconv(x, w, b):
    K, C = w.shape
    y = lax.conv_general_dilated(x, w[:, None, :].astype(x.dtype), window_strides=(1,),
                                 padding=[(K - 1, 0)], dimension_numbers=('NWC', 'WIO', 'NWC'),
                                 feature_group_count=C)
    return y + b


def _t5_bucket(rel):
    nb = REL_BUCKETS // 2
    max_exact = nb // 2
    bucket = jnp.where(rel > 0, nb, 0)
    n = jnp.abs(rel)
    nf = jnp.maximum(n, 1).astype(jnp.float32)
    large = max_exact + (jnp.log(nf / max_exact) / math.log(REL_MAX_DIST / max_exact)
                         * (nb - max_exact)).astype(jnp.int32)
    large = jnp.minimum(large, nb - 1)
    return bucket + jnp.where(n < max_exact, n, large)


def _ssd_scan(X, a, Bm, Cm):
    Bsz, L, H, P = X.shape
    nc = L // SSD_CHUNK
    E = H // SSD_GROUPS
    X = X.reshape(Bsz, nc, SSD_CHUNK, SSD_GROUPS, E, P)
    a = a.reshape(Bsz, nc, SSD_CHUNK, SSD_GROUPS, E).transpose(0, 3, 4, 1, 2)
    Bm = Bm.reshape(Bsz, nc, SSD_CHUNK, SSD_GROUPS, SSD_STATE)
    Cm = Cm.reshape(Bsz, nc, SSD_CHUNK, SSD_GROUPS, SSD_STATE)
    a_cum = jnp.cumsum(a.astype(jnp.float32), axis=-1)
    causal = jnp.tril(jnp.ones((SSD_CHUNK, SSD_CHUNK), dtype=bool))
    seg = a_cum[..., :, None] - a_cum[..., None, :]
    decay = jnp.exp(jnp.where(causal, seg, -jnp.inf))
    cb = jnp.einsum('bclgn,bcsgn->bgcls', Cm, Bm)
    y_diag = jnp.einsum('bgecls,bcsgep->bclgep', cb[:, :, None] * decay, X)
    decay_states = jnp.exp(a_cum[..., -1:] - a_cum).transpose(0, 3, 4, 1, 2)
    states = jnp.einsum('bclgn,bclgep->cbgepn', Bm, X * decay_states[..., None])
    chunk_decay = jnp.exp(a_cum[..., -1]).transpose(3, 0, 1, 2)

    def step(carry, inp):
        st, dec = inp
        return carry * dec[..., None, None] + st, carry

    init = jnp.zeros(states.shape[1:], states.dtype)
    _, prev = lax.scan(step, init, (states, chunk_decay))
    y_off = jnp.einsum('bclgn,cbgepn->bclgep', Cm, prev) * \
        jnp.exp(a_cum).transpose(0, 3, 4, 1, 2)[..., None]
    return (y_diag + y_off).reshape(Bsz, L, H, P)


def _ssd_branch(z, xbc, dt, conv_w, conv_b, dt_bias, A_log, D_skip, norm_g):
    Bsz, L, _ = z.shape
    xbc = jax.nn.silu(_causal_dwconv(xbc, conv_w, conv_b))
    xs, Bm, Cm = jnp.split(xbc, [SSD_INNER, SSD_INNER + SSD_GROUPS * SSD_STATE], axis=-1)
    xs = xs.reshape(Bsz, L, SSD_HEADS, SSD_HEADDIM)
    Bm = Bm.reshape(Bsz, L, SSD_GROUPS, SSD_STATE)
    Cm = Cm.reshape(Bsz, L, SSD_GROUPS, SSD_STATE)
    dt = jax.nn.softplus(dt.astype(jnp.float32) + dt_bias.astype(jnp.float32))
    A = -jnp.exp(A_log.astype(jnp.float32))
    y = _ssd_scan(xs * dt[..., None], dt * A, Bm, Cm)
    y = (y + xs * D_skip[:, None]).reshape(Bsz, L, SSD_INNER)
    v = (y * jax.nn.silu(z)).astype(jnp.float32).reshape(Bsz, L, SSD_GROUPS, SSD_INNER // SSD_GROUPS)
    v = v * lax.rsqrt(jnp.mean(v * v, -1, keepdims=True) + LN_EPS)
    return (v.reshape(Bsz, L, SSD_INNER) * norm_g).astype(z.dtype)


def _dsa_branch(q, ckv, q_idx, k_idx, w_idx, kv_norm_g, w_uk, w_uv, idxk_g, idxk_b, rel_bias):
    Bsz, L, _ = q.shape
    topk = min(TOPK_MAX, L // 4)
    nb = L // Q_BLOCK
    q = q.reshape(Bsz, L, ATT_HEADS, ATT_HEADDIM)
    ckv = _rmsnorm(ckv, kv_norm_g)
    q_lat = jnp.einsum('bthd,hcd->bthc', q, w_uk) * (ATT_HEADDIM ** -0.5)
    q_idx = q_idx.reshape(Bsz, L, IDX_HEADS, IDX_HEADDIM)
    k_idx = _layernorm(k_idx, idxk_g, idxk_b)
    w_idx = w_idx * (IDX_HEADS ** -0.5 * IDX_HEADDIM ** -0.5)
    pos = jnp.arange(L, dtype=jnp.int32)
    key_chunk = pos // CHUNK

    def blk(a):
        return a.reshape(Bsz, nb, Q_BLOCK, *a.shape[2:]).swapaxes(0, 1)

    def attend(inp):
        ql, qi, wi, qpos = inp
        qchunk = qpos // CHUNK
        s = jax.nn.relu(jnp.einsum('bthd,bsd->bths', qi, k_idx))
        score = jnp.einsum('bths,bth->bts', s, wi).astype(jnp.float32)
        admissible = key_chunk[None, :] <= qchunk[:, None]
        score = jnp.where(admissible[None], score, -jnp.inf)
        _, idx = lax.top_k(score, topk)
        valid = (idx // CHUNK) <= qchunk[None, :, None]
        kv = jax.vmap(lambda c, i: jnp.take(c, i, axis=0))(ckv, idx)
        logits = jnp.einsum('bthc,btkc->bthk', ql, kv).astype(jnp.float32)
        bias = jnp.take(rel_bias, _t5_bucket(idx - qpos[None, :, None]), axis=0)
        logits = logits + bias.astype(jnp.float32).transpose(0, 1, 3, 2)
        logits = jnp.where(valid[:, :, None, :], logits, -jnp.inf)
        p = jax.nn.softmax(logits, axis=-1).astype(kv.dtype)
        o_lat = jnp.einsum('bthk,btkc->bthc', p, kv)
        o = jnp.einsum('bthc,hcd->bthd', o_lat, w_uv)
        return o.reshape(o.shape[0], Q_BLOCK, ATT_HEADS * ATT_HEADDIM)

    out = lax.map(attend, (blk(q_lat), blk(q_idx), blk(w_idx), pos.reshape(nb, Q_BLOCK)))
    return out.swapaxes(0, 1).reshape(Bsz, L, ATT_HEADS * ATT_HEADDIM)


def _mixing(h, w_in, conv_w, conv_b, dt_bias, A_log, D_skip, ssd_norm_g,
            kv_norm_g, w_uk, w_uv, idxk_g, idxk_b, rel_bias, w_br_ssd, w_br_att, w_out):
    proj = h @ w_in
    z, xbc, dt, q, ckv, q_idx, k_idx, w_idx, g_ssd, g_att = jnp.split(proj, _split_points(), axis=-1)
    y_ssd = _ssd_branch(z, xbc, dt, conv_w, conv_b, dt_bias, A_log, D_skip, ssd_norm_g)
    y_att = _dsa_branch(q, ckv, q_idx, k_idx, w_idx, kv_norm_g, w_uk, w_uv, idxk_g, idxk_b, rel_bias)
    m = jax.nn.sigmoid(g_ssd) * (y_ssd @ w_br_ssd) + jax.nn.sigmoid(g_att) * (y_att @ w_br_att)
    return m @ w_out


def _conv_ffn(h, w_up, conv_w, conv_b, w_down):
    u = _causal_dwconv(h @ w_up, conv_w, conv_b)
    a, v = jnp.split(u, 2, axis=-1)
    return (jax.nn.silu(a) * v) @ w_down


def setup_inputs(seed: int = 0) -> dict:
    key = jax.random.key(seed)
    ks = iter(jax.random.split(key, 40))

    def nrm(shape, scale):
        return scale * jax.random.normal(next(ks), shape, jnp.float32)

    def gain(shape):
        return 1.0 + nrm(shape, 0.02)

    n_in = sum(_in_split_sizes())
    dt0 = jnp.exp(jax.random.uniform(next(ks), (DEPTH, SSD_HEADS), jnp.float32)
                  * (math.log(0.1) - math.log(0.001)) + math.log(0.001))
    dt_bias = dt0 + jnp.log(-jnp.expm1(-dt0))
    A_log = jnp.log(jax.random.uniform(next(ks), (DEPTH, SSD_HEADS), jnp.float32, 1.0, 16.0))
    return {
        "x": nrm((BATCH, SEQ, D_MODEL), 1.0),
        "ln_in_g": gain((D_MODEL,)),
        "ln_in_b": nrm((D_MODEL,), 0.02),
        "w_in": nrm((DEPTH, D_MODEL, n_in), D_MODEL ** -0.5),
        "ssd_conv_w": nrm((DEPTH, SSD_CONV, SSD_XBC), SSD_CONV ** -0.5),
        "ssd_conv_b": nrm((DEPTH, SSD_XBC), 0.01),
        "ssd_dt_bias": dt_bias,
        "ssd_A_log": A_log,
        "ssd_D": 1.0 + nrm((DEPTH, SSD_HEADS), 0.1),
        "ssd_norm_g": gain((DEPTH, SSD_INNER)),
        "att_kv_norm_g": gain((DEPTH, ATT_LATENT)),
        "att_w_uk": nrm((DEPTH, ATT_HEADS, ATT_LATENT, ATT_HEADDIM), ATT_LATENT ** -0.5),
        "att_w_uv": nrm((DEPTH, ATT_HEADS, ATT_LATENT, ATT_HEADDIM), ATT_LATENT ** -0.5 * DN_BETA),
        "idx_k_norm_g": gain((DEPTH, IDX_HEADDIM)),
        "idx_k_norm_b": nrm((DEPTH, IDX_HEADDIM), 0.02),
        "rel_bias": nrm((REL_BUCKETS, ATT_HEADS), 0.2),
        "w_br_ssd": nrm((DEPTH, SSD_INNER, D_MODEL), SSD_INNER ** -0.5 * DN_BETA),
        "w_br_att": nrm((DEPTH, ATT_HEADS * ATT_HEADDIM, D_MODEL), (ATT_HEADS * ATT_HEADDIM) ** -0.5 * DN_BETA),
        "w_out": nrm((DEPTH, D_MODEL, D_MODEL), D_MODEL ** -0.5 * DN_BETA),
        "ln1_g": gain((DEPTH, D_MODEL)),
        "ln1_b": nrm((DEPTH, D_MODEL), 0.02),
        "ffn_w_up": nrm((DEPTH, D_MODEL, 2 * D_FF), D_MODEL ** -0.5 * DN_BETA),
        "ffn_conv_w": nrm((DEPTH, FFN_CONV, 2 * D_FF), FFN_CONV ** -0.5),
        "ffn_conv_b": nrm((DEPTH, 2 * D_FF), 0.01),
        "ffn_w_down": nrm((DEPTH, D_FF, D_MODEL), D_FF ** -0.5 * DN_BETA),
        "ln2_g": gain((DEPTH, D_MODEL)),
        "ln2_b": nrm((DEPTH, D_MODEL), 0.02),
    }


def reference(x, ln_in_g, ln_in_b, w_in, ssd_conv_w, ssd_conv_b, ssd_dt_bias, ssd_A_log, ssd_D,
              ssd_norm_g, att_kv_norm_g, att_w_uk, att_w_uv, idx_k_norm_g, idx_k_norm_b, rel_bias,
              w_br_ssd, w_br_att, w_out, ln1_g, ln1_b, ffn_w_up, ffn_conv_w, ffn_conv_b, ffn_w_down,
              ln2_g, ln2_b):
    h = _layernorm(x, ln_in_g, ln_in_b)
    for l in range(DEPTH):
        mix = _mixing(h, w_in[l], ssd_conv_w[l], ssd_conv_b[l], ssd_dt_bias[l], ssd_A_log[l], ssd_D[l],
                      ssd_norm_g[l], att_kv_norm_g[l], att_w_uk[l], att_w_uv[l], idx_k_norm_g[l],
                      idx_k_norm_b[l], rel_bias, w_br_ssd[l], w_br_att[l], w_out[l])
        h = _layernorm(DN_ALPHA * h + mix.astype(h.dtype), ln1_g[l], ln1_b[l])
        f = _conv_ffn(h, ffn_w_up[l], ffn_conv_w[l], ffn_conv_b[l], ffn_w_down[l])
        h = _layernorm(DN_ALPHA * h + f.astype(h.dtype), ln2_g[l], ln2_b[l])
    return h.astype(x.dtype)
```

```python
import math
from contextlib import ExitStack
import numpy as np
import concourse.bass as bass
import concourse.mybir as mybir
from concourse.bass_utils import run_bass_kernel_spmd

F32 = mybir.dt.float32
BF16 = mybir.dt.bfloat16
U8 = mybir.dt.uint8
ALU = mybir.AluOpType
AF = mybir.ActivationFunctionType
AX = mybir.AxisListType

ENGS = ['sync', 'scalar', 'vector', 'gpsimd', 'tensor']
NDS = 8


def _acc(x):
    if isinstance(x, (str, tuple)):
        return (x, 0, 1 << 30, 0, 1 << 60)
    name = x.name
    if not hasattr(x, 'offset') or not hasattr(x, 'ap'):
        return (name, 0, 1 << 30, 0, 1 << 60)
    sz = mybir.dt.size(x.dtype)
    ap = x.ap
    off = x.offset
    if name.startswith('ps'):
        return (name, 0, 1 << 30, 0, 1 << 60)
    if name.startswith('sb_'):
        pitch = ap[0][0]
        if pitch <= 0:
            return (name, 0, 1 << 30, 0, 1 << 60)
        plo = off // pitch
        phi = plo + ap[0][1]
        lo = off % pitch
        hi = lo + sum((c - 1) * abs(st_) for st_, c in ap[1:]) + 1
        return (name, plo, phi, lo * sz, hi * sz)
    hi = off + sum((c - 1) * abs(st_) for st_, c in ap) + 1
    return (name, 0, 1, off * sz, hi * sz)


def _ovl(a, b):
    return a[1] < b[2] and b[1] < a[2] and a[3] < b[4] and b[3] < a[4]


def _cov(a, b):
    return a[1] <= b[1] and b[2] <= a[2] and a[3] <= b[3] and b[4] <= a[4]


class Prog:
    def __init__(self, nc, es):
        self.nc = nc
        self.ops = {e: [] for e in ENGS}
        self.cnt = {e: 0 for e in ENGS}
        self.W = {}
        self.R = {}
        self.waited = {e: {} for e in ENGS}
        self.dma_n = {e: 0 for e in ENGS}
        self.dma_tok = {e: [] for e in ENGS}
        self.sems = {}
        self.es = es
        self.all_dma = []

    def sem(self, k):
        if k not in self.sems:
            self.sems[k] = self.es.enter_context(self.nc.semaphore("s_" + "_".join(str(a) for a in k)))
        return self.sems[k]

    def add(self, eng, fn, reads=(), writes=(), dma=False):
        raw = set()
        oth = set()
        ra = [_acc(k) for k in reads]
        wa = [_acc(k) for k in writes]
        for a in ra:
            for w in self.W.get(a[0], ()):
                if _ovl(w, a):
                    raw.add(w[5])
        for a in wa:
            for w in self.W.get(a[0], ()):
                if _ovl(w, a):
                    oth.add(w[5])
            for r in self.R.get(a[0], ()):
                if _ovl(r, a):
                    oth.add(r[5])
        if dma:
            n = self.dma_n[eng]
            self.dma_n[eng] += 1
            tok = (('d', eng, n % NDS), 16 * (n // NDS + 1), eng, True)
            if n >= NDS:
                raw.add(self.dma_tok[eng][n - NDS])
            self.dma_tok[eng].append(tok)
            self.all_dma.append(tok)
        else:
            self.cnt[eng] += 1
            if self.cnt[eng] % 60000 == 0:
                self.cnt[eng] += 1
            tok = (('c', eng, self.cnt[eng] // 60000), self.cnt[eng] % 60000, eng, False)
        deps = set(raw)
        for d in oth:
            if (not dma) and (not d[3]) and d[2] == eng and eng == 'tensor':
                continue
            deps.add(d)
        waits = {}
        for (s, v, _, _) in deps:
            if self.waited[eng].get(s, 0) >= v:
                continue
            waits[s] = max(waits.get(s, 0), v)
        for s, v in waits.items():
            self.waited[eng][s] = v
        self.ops[eng].append((list(waits.items()), fn, tok))
        for a in wa:
            e = a + (tok,)
            self.W[a[0]] = [w for w in self.W.get(a[0], ()) if not _cov(a, w)] + [e]
            self.R[a[0]] = [r for r in self.R.get(a[0], ()) if not _cov(a, r)]
        for a in ra:
            e = a + (tok,)
            lst = self.R.get(a[0], [])
            if not dma:
                lst = [r for r in lst if not (r[5][2] == eng and not r[5][3] and _cov(a, r))]
            lst.append(e)
            self.R[a[0]] = lst
        return tok

    def dma(self, eng, out, in_, reads=None, writes=None, **kw):
        return self.add(eng, lambda e: e.dma_start(out=out, in_=in_, **kw),
                        reads=[in_] if reads is None else reads,
                        writes=[out] if writes is None else writes, dma=True)

    def emit(self):
        nc = self.nc
        fin = {}
        for tok in self.all_dma:
            fin[tok[0]] = max(fin.get(tok[0], 0), tok[1])
        for e in ENGS:
            if self.cnt[e] > 0:
                fin[('c', e, self.cnt[e] // 60000)] = self.cnt[e] % 60000
        for k in list(fin) + [w for e in ENGS for (ws, _, t) in self.ops[e] for w in [t[0]] + [a for a, _ in ws]]:
            self.sem(k)
        with nc.Block() as block:
            def run(engname):
                def body(e):
                    for (waits, fn, tok) in self.ops[engname]:
                        for s, v in waits:
                            e.wait_ge(self.sem(s), v)
                        ins = fn(e)
                        ins.then_inc(self.sem(tok[0]), 16 if tok[3] else 1)
                    if engname == 'sync':
                        for s, v in fin.items():
                            e.wait_ge(self.sem(s), v)
                return body
            block.sync(run('sync'))
            block.scalar(run('scalar'))
            block.vector(run('vector'))
            block.gpsimd(run('gpsimd'))
            block.tensor(run('tensor'))


D_MODEL = 1024
SEQ = 4096
N_IN = 9064
D_FF = 2816
ALPHA = 2.0 ** 0.25
EPS = 1e-5
NEG = -30000.0
NBIS = 20
OPT = {}
CHKT = 2
BIS_HW = 256.0

O_Z, O_XBC, O_DT, O_Q, O_CKV, O_QI, O_KI, O_WI, O_GS, O_GA = 0, 2048, 5120, 5152, 6176, 6432, 6944, 7008, 7016, 8040


def _t5_bucket(rel):
    nb = 16
    max_exact = 8
    bucket = np.where(rel > 0, nb, 0)
    n = np.abs(rel)
    nf = np.maximum(n, 1).astype(np.float32)
    large = max_exact + (np.log(nf / max_exact) / math.log(128 / max_exact) * (nb - max_exact)).astype(np.int32)
    large = np.minimum(large, nb - 1)
    return bucket + np.where(n < max_exact, n, large)


def _blk(W, cols, N):
    K = W.shape[0]
    cols = np.asarray(cols)
    nb = len(cols) // N
    Wc = W[:, cols].reshape(K // 128, 128, nb, N)
    return np.ascontiguousarray(Wc.transpose(2, 1, 0, 3)).astype(np.float32)


def host_consts():
    c = {}
    p = np.arange(128)
    same = (p[:, None] // 64) == (p[None, :] // 64)
    c['Tm'] = (same & (p[:, None] <= p[None, :])).astype(np.float32)
    c['U'] = (same & (p[:, None] > p[None, :])).astype(np.float32)
    ob = np.zeros((128, 256), np.float32)
    ob[:64, :128] = 1.0
    ob[64:, 128:] = 1.0
    c['onesblk'] = ob
    c['cmneg'] = np.where((p[None, :] // 64) <= (p[:, None] // 64), 0.0, NEG).astype(np.float32)
    c['identf'] = np.eye(128, dtype=np.float32)
    j = np.arange(256)
    oh = np.zeros((2, 32, 256), np.float32)
    for k, rel in enumerate([-1 - j, 127 - j]):
        b = _t5_bucket(rel)
        oh[k, b, j] = 1.0
        oh[k, 15, :] -= 1.0
    c['ohb'] = oh
    c['ones32'] = np.ones((32, 128), np.float32)
    return c


def host_weights(inp):
    w = {}
    w_in = np.asarray(inp['w_in'][0], np.float32)
    w['wsm'] = _blk(w_in, list(range(O_DT, O_DT + 32)) + list(range(O_CKV, O_CKV + 256)) +
                    list(range(O_KI, O_KI + 64)) + list(range(O_WI, O_WI + 8)), 360)
    w['wz'] = _blk(w_in, range(O_Z, O_Z + 2048), 512)
    w['wxs'] = _blk(w_in, range(O_XBC, O_XBC + 2048), 512)
    bc = []
    for g in range(4):
        bc += list(range(O_XBC + 2048 + g * 128, O_XBC + 2048 + (g + 1) * 128))
        bc += list(range(O_XBC + 2560 + g * 128, O_XBC + 2560 + (g + 1) * 128))
    w['wbc'] = _blk(w_in, bc, 256)
    w['wq'] = _blk(w_in, range(O_Q, O_Q + 1024), 512)
    w['wqi'] = _blk(w_in, range(O_QI, O_QI + 512), 512)
    w['wgs'] = _blk(w_in, range(O_GS, O_GS + 1024), 512)
    w['wga'] = _blk(w_in, range(O_GA, O_GA + 1024), 512)
    w['wbs'] = _blk(np.asarray(inp['w_br_ssd'][0], np.float32), range(1024), 256)
    w['wba'] = _blk(np.asarray(inp['w_br_att'][0], np.float32), range(1024), 512)
    w['wo'] = _blk(np.asarray(inp['w_out'][0], np.float32), range(1024), 512)
    up = []
    for j in range(22):
        up += list(range(j * 128, (j + 1) * 128)) + list(range(D_FF + j * 128, D_FF + (j + 1) * 128))
    w['wup'] = _blk(np.asarray(inp['ffn_w_up'][0], np.float32), up, 512)
    w['wdn'] = _blk(np.asarray(inp['ffn_w_down'][0], np.float32), range(1024), 128)
    return w


W_SHAPES = {'wsm': (1, 8, 360), 'wz': (4, 8, 512), 'wxs': (4, 8, 512), 'wbc': (4, 8, 256), 'wq': (2, 8, 512),
            'wqi': (1, 8, 512), 'wgs': (2, 8, 512), 'wga': (2, 8, 512), 'wbs': (4, 16, 256), 'wba': (2, 8, 512),
            'wo': (2, 8, 512), 'wup': (11, 8, 512), 'wdn': (8, 22, 128)}


def host_small(inp):
    s = {}
    f = lambda a: np.ascontiguousarray(np.asarray(a, np.float32))
    s['ln0'] = f(np.stack([inp['ln_in_g'], inp['ln_in_b']])[None])
    s['ln1'] = f(np.stack([inp['ln1_g'][0], inp['ln1_b'][0]])[None])
    s['ln2'] = f(np.stack([inp['ln2_g'][0], inp['ln2_b'][0]])[None])
    cw = np.asarray(inp['ssd_conv_w'][0], np.float32)
    s['cw'] = f(cw.T.reshape(24, 128, 4).transpose(1, 0, 2))
    s['cb'] = f(np.asarray(inp['ssd_conv_b'][0]).reshape(24, 128).T)
    fw = np.asarray(inp['ffn_conv_w'][0], np.float32)
    s['fcw'] = f(fw.T.reshape(44, 128, 3).transpose(1, 0, 2))
    s['fcb'] = f(np.asarray(inp['ffn_conv_b'][0]).reshape(44, 128).T)
    s['hp'] = f(np.stack([inp['ssd_dt_bias'][0], inp['ssd_A_log'][0], inp['ssd_D'][0]])[None])
    s['ng'] = f(np.asarray(inp['ssd_norm_g'][0]).reshape(16, 128).T)
    s['kvg'] = f(np.asarray(inp['att_kv_norm_g'][0])[None])
    s['ikp'] = f(np.stack([inp['idx_k_norm_g'][0], inp['idx_k_norm_b'][0]])[None])
    uk = np.asarray(inp['att_w_uk'][0], np.float32)
    s['wukT'] = f(uk.transpose(0, 2, 1).reshape(8, 128, 256).transpose(1, 0, 2))
    uv = np.asarray(inp['att_w_uv'][0], np.float32)
    s['wuv'] = f(uv.transpose(1, 0, 2).reshape(2, 128, 1024).transpose(1, 0, 2))
    s['relb'] = f(inp['rel_bias'])
    return s


S_SHAPES = {'ln0': (1, 2, 1024), 'ln1': (1, 2, 1024), 'ln2': (1, 2, 1024), 'cw': (128, 24, 4), 'cb': (128, 24),
            'fcw': (128, 44, 3), 'fcb': (128, 44), 'hp': (1, 3, 32), 'ng': (128, 16), 'kvg': (1, 256),
            'ikp': (1, 2, 64), 'wukT': (128, 8, 256), 'wuv': (128, 2, 1024), 'relb': (32, 16)}
C_SHAPES = {'Tm': (128, 128), 'U': (128, 128), 'onesblk': (128, 256), 'cmneg': (128, 128), 'identf': (128, 128),
            'ohb': (2, 32, 256), 'ones32': (32, 128)}


class K:
    def __init__(self, P):
        self.P = P

    @staticmethod
    def _aps(xs):
        return [x for x in xs if hasattr(x, 'name') and not isinstance(x, (str, float, int))]

    def mm(self, out, lhsT, rhs, start=True, stop=True):
        self.P.add('tensor', lambda e: e.matmul(out, lhsT, rhs, start=start, stop=stop),
                   reads=[lhsT, rhs], writes=[out])

    def tr(self, out, in_, ident):
        self.P.add('tensor', lambda e: e.transpose(out, in_, ident), reads=[in_, ident], writes=[out])

    def act(self, out, in_, func, bias=0.0, scale=1.0, accum=None):
        kw = {}
        if accum is not None:
            kw['accum_out'] = accum
        self.P.add('scalar', lambda e: e.activation(out, in_, func, bias=bias, scale=scale, **kw),
                   reads=self._aps([in_, bias, scale]), writes=self._aps([out, accum]))

    def ts(self, eng, out, in0, s1, s2, op0, op1=None, accum=None):
        kw = {}
        if accum is not None:
            kw['accum_out'] = accum
        if op1 is None:
            op1 = ALU.bypass
        self.P.add(eng, lambda e: e.tensor_scalar(out, in0, s1, s2, op0, op1, **kw),
                   reads=self._aps([in0, s1, s2]), writes=self._aps([out, accum]))

    def tt(self, eng, out, in0, in1, op):
        self.P.add(eng, lambda e: e.tensor_tensor(out, in0, in1, op), reads=[in0, in1], writes=[out])

    def stt(self, out, in0, scalar, in1, op0, op1):
        self.P.add('vector', lambda e: e.scalar_tensor_tensor(out, in0, scalar, in1, op0, op1),
                   reads=self._aps([in0, scalar, in1]), writes=[out])

    def copy(self, eng, out, in_):
        if eng == 'scalar':
            self.P.add('scalar', lambda e: e.copy(out, in_), reads=[in_], writes=[out])
        else:
            self.P.add(eng, lambda e: e.tensor_copy(out, in_), reads=[in_], writes=[out])

    def memset(self, eng, ap, v):
        self.P.add(eng, lambda e: e.memset(ap, v), writes=[ap])

    def recip(self, out, in_):
        self.P.add('vector', lambda e: e.reciprocal(out, in_), reads=[in_], writes=[out])

    def bn_stats(self, out, in_):
        self.P.add('vector', lambda e: e.bn_stats(out, in_), reads=[in_], writes=[out])

    def bn_aggr(self, out, in_):
        self.P.add('vector', lambda e: e.bn_aggr(out, in_), reads=[in_], writes=[out])


class _Stop(Exception):
    pass


def build(L, dbg=(), limit=99):
    NT = L // 128
    NST = L // 512
    nc = bass.Bass("TRN2", target_bir_lowering=False)

    def din(name, shape):
        return nc.dram_tensor(name, list(shape), F32, kind="ExternalInput").ap()

    x_d = din("x", [L, 1024])
    wd = {k: din(k, [v[0] * 128, v[1] * v[2]]) for k, v in W_SHAPES.items()}
    wbf = {k: nc.dram_tensor(k + "_bf", [v[0] * 128, v[1] * v[2]], BF16, kind="Internal").ap()
           for k, v in W_SHAPES.items()}
    sd = {k: din(k, v) for k, v in S_SHAPES.items()}
    cd = {k: din(k, v) for k, v in C_SHAPES.items()}
    out_d = nc.dram_tensor("out", [L, 1024], F32, kind="ExternalOutput").ap()
    zd_t = nc.dram_tensor("zscr", [2, 128, 4096], F32, kind="Internal")
    zd = zd_t.ap()
    dbg_d = {}

    es = ExitStack()
    with es:
        def T(name, shape, dt):
            return es.enter_context(nc.sbuf_tensor("sb_" + name, list(shape), dt))
        P = Prog(nc, es)
        k = K(P)
        V, G, A = 'vector', 'gpsimd', 'scalar'

        identb = T("identb", [128, 128], BF16)
        Tm = T("Tm", [128, 128], F32)
        U = T("U", [128, 128], F32)
        onesblk = T("onesblk", [128, 256], F32)
        cmneg = T("cmneg", [128, 128], F32)
        lnp = T("lnp", [128, 2, 1024], F32)
        cw = T("cw", [128, 24, 4], F32)
        cb = T("cb", [128, 24], F32)
        fcw = T("fcw", [128, 44, 3], F32)
        fcb = T("fcb", [128, 44], F32)
        hpb = T("hpb", [128, 3, 32], F32)
        Aneg = T("Aneg", [128, 32], F32)
        ng = T("ng", [128, 16], F32)
        kvg = T("kvg", [128, 256], F32)
        ikp = T("ikp", [128, 2, 64], F32)
        wukT = T("wukT", [128, 8, 256], BF16)
        wuv = T("wuv", [128, 2, 1024], BF16)
        biasT = T("biasT", [128, 16, 256], BF16)
        kvnT = T("kvnT", [128, 2, L], BF16)
        kvtok = T("kvtok", [128, NT, 258], BF16)
        kidxT = T("kidxT", [128, L], BF16)
        S = T("S", [128, 4, 512], F32)
        hal = T("hal", [128, 24, 3], F32)
        fhal = T("fhal", [128, 44, 2], F32)
        xt = T("xt", [128, 1024], F32)
        hb = T("hb", [128, 1024], BF16)
        hT = T("hT", [128, 8, 512], BF16)
        wb = [T("wb0", [128, 4096], BF16), T("wb1", [128, 4096], BF16)]
        s16 = T("s16", [128, 4096], F32)
        ybuf = T("ybuf", [128, 24, 512], BF16)
        sA = T("sA", [128, 4096], BF16)
        sB = T("sB", [128, 4096], BF16)
        sC = T("sC", [128, 4096], BF16)
        sD = T("sD", [128, 4096], BF16)
        fs = [T("fs%d" % i, [128, 516], F32) for i in range(4)]
        junk = T("junk", [128, max(L, 4096)], U8)
        st = T("st", [128, 64], F32)
        dts = T("dts", [128, 4, 32], F32)
        wi = T("wi", [128, 4, 8], F32)
        qTm = xt[:].bitcast(BF16).rearrange("p (a k t) -> p a k t", a=2, k=8)
        qiTm = lnp[:].rearrange("p a b -> p (a b)").bitcast(BF16)[:, 0:1024].rearrange("p (a k t) -> p a k t", a=2, k=4)
        maskT2 = T("maskT2", [128, 4096], BF16)
        _m2f = maskT2[:].bitcast(F32)
        fs.append(_m2f[:, 0:516])
        fs.append(_m2f[:, 516:1032])
        sst = [T("sst%d" % i, [128, 64], F32) for i in range(2)]
        cbms = [T("cbm%d" % i, [128, 128], F32) for i in range(2)]
        pmask = T("pmask", [128, 2], F32)
        pw2 = T("pw2", [128, 64], F32)
        bst = T("bst", [128, 64], F32)
        ps = [es.enter_context(nc.psum_tensor("ps%d" % i, [128, 512], F32)) for i in range(8)]
        psn = [0]

        psmod = [8]

        def nps():
            psn[0] = (psn[0] + 1) % psmod[0]
            return ps[psn[0]]

        def psb(p):
            return p[:].bitcast(BF16)

        def dump(name, ap, shape, dt=F32):
            if name in dbg:
                d = nc.dram_tensor("dbg_" + name, list(shape), dt, kind="ExternalOutput").ap()
                P.dma('sync', d, ap)

        wbn = [0]

        def load_w(name, blk, eng='sync'):
            nb, kc, n = W_SHAPES[name]
            wbn[0] ^= 1
            t = wb[wbn[0]]
            P.dma(eng, t[:, 0:kc * n], wbf[name][blk * 128:(blk + 1) * 128, :])
            return t[:, 0:kc * n].rearrange("p (k n) -> p k n", n=n)

        for t_, nm in ((Tm, 'Tm'), (U, 'U'), (onesblk, 'onesblk'), (cmneg, 'cmneg'), (cw, 'cw'), (cb, 'cb'),
                       (fcw, 'fcw'), (fcb, 'fcb'), (ng, 'ng')):
            P.dma('sync', t_[:], cd[nm] if nm in cd else sd[nm])
        P.dma('sync', hpb[:], sd['hp'].partition_broadcast(128)[:, 0])
        P.dma('sync', kvg[:], sd['kvg'].partition_broadcast(128)[:, 0])
        P.dma('sync', ikp[:], sd['ikp'].partition_broadcast(128)[:, 0])
        P.dma('sync', s16[:, 0:128], cd['identf'])
        k.copy(V, identb[:], s16[:, 0:128])
        k.act(Aneg[:], hpb[:, 1, :], AF.Exp)
        k.ts(V, Aneg[:], Aneg[:], -1.0, None, ALU.mult)
        P.dma('sync', s16[:, 0:2048], sd['wukT'].rearrange("p a b -> p (a b)"))
        k.copy(V, wukT[:].rearrange("p a b -> p (a b)"), s16[:, 0:2048])
        P.dma('sync', s16[:, 2048:4096], sd['wuv'].rearrange("p a b -> p (a b)"))
        k.copy(G, wuv[:].rearrange("p a b -> p (a b)"), s16[:, 2048:4096])
        k.memset(V, hal[:], 0.0)
        k.memset(V, fhal[:], 0.0)
        k.memset(V, S[:], 0.0)
        k.memset(G, kvtok[:, :, 256:258], 1.0)
        k.memset(V, pmask[:], 0.0)
        k.memset(V, pmask[0:64, 0:1], 1.0)
        k.memset(V, pmask[64:128, 1:2], 1.0)
        for it in range(NBIS + 1):
            k.memset(G, pw2[:, it:it + 1], 2.0 ** (-it))
        rb = fs[0][0:32, 0:16]
        P.dma('sync', rb, sd['relb'])
        oh = fs[1][0:32, 0:512].rearrange("p (a b) -> p a b", a=2)
        P.dma('sync', oh, cd['ohb'].rearrange("a p b -> p a b"))
        o32 = fs[2][0:32, 0:128]
        P.dma('sync', o32, cd['ones32'])
        ybf = ybuf[:].rearrange("p a b -> p (a b)").bitcast(F32)
        for kk in range(2):
            rhsb = ybf[0:32, 0:4096].rearrange("p (h j) -> p h j", h=16)
            for h in range(16):
                k.ts(V, rhsb[:, h, :], oh[:, kk, :], rb[:, h:h + 1], None, ALU.mult)
            zs = s16[:, 0:4096]
            for b in range(8):
                p_ = nps()
                k.mm(p_[:], o32, ybf[0:32, b * 512:(b + 1) * 512])
                k.copy(V, zs[:, b * 512:(b + 1) * 512], p_[:])
            P.dma('sync', zd[kk], zs)
            src = bass.AP(tensor=zd_t, offset=kk * 128 * 4096 + 127, ap=[[4095, 128], [256, 16], [1, 128]])
            stg = ybf[:, 4096:6144].rearrange("p (h t) -> p h t", h=16)
            P.dma('sync', stg, src, reads=['zscr'], writes=[ybuf])
            k.ts(V, biasT[:, :, kk * 128:(kk + 1) * 128], stg, 8.0, None, ALU.mult)
        ci = 0
        for name, (nb, kc, n) in W_SHAPES.items():
            for b in range(nb):
                stg = s16[:, 0:kc * n] if ci % 2 == 0 else ybf[:, 0:kc * n]
                P.dma('sync' if ci % 2 == 0 else 'scalar', stg, wd[name][b * 128:(b + 1) * 128, :])
                wbn[0] ^= 1
                o = wb[wbn[0]][:, 0:kc * n]
                k.copy([V, G, A][ci % 3], o, stg)
                P.dma('gpsimd', wbf[name][b * 128:(b + 1) * 128, :], o)
                ci += 1

        def rstd_from(var_ap, out_ap, scale, eps):
            k.ts(V, out_ap, var_ap, scale, eps, ALU.mult, ALU.add)
            k.act(out_ap, out_ap, AF.Ln)
            k.act(out_ap, out_ap, AF.Exp, scale=-0.5)

        def layernorm(src, dst, dstb=None):
            k.bn_stats(st[:, 0:6], src[:, 0:512])
            k.bn_stats(st[:, 6:12], src[:, 512:1024])
            k.bn_aggr(st[:, 12:14], st[:, 0:12].rearrange("p (a b) -> p a b", a=2))
            rstd_from(st[:, 13:14], st[:, 14:15], 1.0, EPS)
            k.ts(V, src, src, st[:, 12:13], st[:, 14:15], ALU.subtract, ALU.mult)
            k.tt(V, src, src, lnp[:, 0, :], ALU.mult)
            if dst is not None:
                k.tt(V, dst, src, lnp[:, 1, :], ALU.add)
                if dstb is not None:
                    k.copy(A, dstb, dst)
            else:
                k.tt(V, dstb, src, lnp[:, 1, :], ALU.add)

        def to_hT(i):
            p_ = nps()
            pb = psb(p_)
            for kc in range(8):
                k.tr(pb[:, kc * 128:(kc + 1) * 128], hb[:, kc * 128:(kc + 1) * 128], identb[:])
            k.copy(A, hT[:, :, i * 128:(i + 1) * 128], pb[:, 0:1024].rearrange("p (k t) -> p k t", k=8))

        def load_ln(name):
            P.dma('sync', lnp[:].rearrange("p a b -> p (a b)"),
                  sd[name].rearrange("o a b -> o (a b)").partition_broadcast(128)[:, 0])

        cvn = [0]

        def conv(p_in, cidx, out_ap, halo, wts, bias, ntap, func):
            h = ntap - 1
            cvn[0] ^= 1
            cbuf = fs[0][:, :] if cvn[0] else fs[4]
            acc = fs[1][:, :] if cvn[0] else fs[5]
            k.copy(V, cbuf[:, 0:h], halo[:, cidx, :])
            k.copy(A, cbuf[:, h:h + 512], p_in)
            k.act(acc[:, 0:512], p_in, AF.Identity, bias=bias[:, cidx:cidx + 1], scale=wts[:, cidx, h:h + 1])
            k.copy(G, halo[:, cidx, :], cbuf[:, 512:512 + h])
            for tp in range(0, h):
                last = (tp == h - 1) and func is None
                k.stt(out_ap if last else acc[:, 0:512], cbuf[:, tp:tp + 512], wts[:, cidx, tp:tp + 1],
                      acc[:, 0:512], ALU.mult, ALU.add)
            if func is not None:
                k.act(out_ap, acc[:, 0:512], func)

        def fm_proj(wv, m, p_):
            for kc in range(8):
                k.mm(p_[:], wv[:, kc, m * 128:(m + 1) * 128], hT[:, kc, :], start=(kc == 0), stop=(kc == 7))

        szg = sA[:].bitcast(F32).rearrange("p (i c) -> p i c", i=4)
        qT = sA[:].rearrange("p (k t) -> p k t", k=8)
        sgs = sA[:, 0:2048].rearrange("p (k t) -> p k t", k=4)
        sga = sA[:, 2048:4096].rearrange("p (k t) -> p k t", k=4)
        sBf = sB[:].bitcast(F32)
        rhsD = sBf[:, 0:1024]
        expD = sBf[:, 1024:2048]
        maskT = sB[:].rearrange("p (k t) -> p k t", t=128)
        Xg = sC[:, 0:2048].rearrange("p (i c) -> p i c", i=4)
        Bg = sC[:, 2048:2560].rearrange("p (i c) -> p i c", i=4)
        BT = sC[:, 2560:3072]
        CT = sC[:, 3072:3584]
        qlT = sC[:].rearrange("p (c h t) -> p c h t", c=2, h=16)
        mT = sC[:].rearrange("p (k t) -> p k t", k=8)
        Gm = sD[:, 0:1024]
        Xdt = sD[:, 1024:1536]
        Xd = sD[:, 1536:2048]
        S0b = sD[:, 2048:2560]
        S1b = sD[:, 2560:3072]
        vn = sD[:, 3072:3584]
        qiT = sD[:, 0:2048].rearrange("p (k t) -> p k t", k=4)
        PTs = [sD[:, 2048:2560], sD[:, 2560:3072]]
        mblk = sD[:, 3072:3584]
        ol = sD[:, 3584:3840]
        oT = sD[:, 3840:4096].rearrange("p (c t) -> p c t", c=2)
        ysT = ybuf[:, 0:16, :]
        yaT = ybuf[:, 16:24, :]
        gT = ybuf
        score = s16
        h1 = s16[:].rearrange("p (i c) -> p i c", i=4)

        def bc(ap, n):
            return ap.unsqueeze(2).to_broadcast([128, ap.shape[1], n])

        def chk(n):
            if limit <= n:
                raise _Stop()

        try:
          chk(1)
          for sti in range(NST):
              load_ln('ln0')
              for i in range(4):
                  Tg = sti * 4 + i
                  P.dma('sync', xt[:], x_d[Tg * 128:(Tg + 1) * 128, :])
                  layernorm(xt[:], None, hb[:])
                  to_hT(i)
              if sti == 0:
                  dump("hT", hT[:], [128, 8, 512], BF16)
              chk(2)
              wv = load_w('wsm', 0)
              for i in range(4):
                  Tg = sti * 4 + i
                  p_ = nps()
                  for kc in range(8):
                      k.mm(p_[:, 0:360], hT[:, kc, i * 128:(i + 1) * 128], wv[:, kc, :], start=(kc == 0), stop=(kc == 7))
                  k.tt(V, st[:, 16:48], p_[:, 0:32], hpb[:, 0, :], ALU.add)
                  k.act(st[:, 16:48], st[:, 16:48], AF.Exp)
                  k.act(dts[:, i, :], st[:, 16:48], AF.Ln, bias=1.0)
                  k.act(fs[2][:, 0:256], p_[:, 32:288], AF.Square, accum=st[:, 48:49])
                  rstd_from(st[:, 48:49], st[:, 49:50], 1.0 / 256, EPS)
                  k.ts(V, fs[2][:, 0:256], p_[:, 32:288], st[:, 49:50], None, ALU.mult)
                  k.tt(V, kvtok[:, Tg, 0:256], fs[2][:, 0:256], kvg[:], ALU.mult)
                  p2 = nps()
                  pb = psb(p2)
                  for cc in range(2):
                      k.tr(pb[:, cc * 128:(cc + 1) * 128], kvtok[:, Tg, cc * 128:(cc + 1) * 128], identb[:])
                  k.copy(A, kvnT[:, :, Tg * 128:(Tg + 1) * 128], pb[:, 0:256].rearrange("p (c t) -> p c t", c=2))
                  k.bn_stats(st[:, 50:56], p_[:, 288:352])
                  k.bn_aggr(st[:, 56:58], st[:, 50:56])
                  rstd_from(st[:, 57:58], st[:, 58:59], 1.0, EPS)
                  k.ts(V, fs[3][:, 0:64], p_[:, 288:352], st[:, 56:57], st[:, 58:59], ALU.subtract, ALU.mult)
                  k.tt(V, fs[3][:, 0:64], fs[3][:, 0:64], ikp[:, 0, :], ALU.mult)
                  k.tt(V, hb[:, 0:64], fs[3][:, 0:64], ikp[:, 1, :], ALU.add)
                  k.copy(V, hb[:, 64:128], hb[:, 0:64])
                  p3 = nps()
                  pb3 = psb(p3)
                  k.tr(pb3[:, 0:128], hb[:, 0:128], identb[:])
                  k.copy(A, kidxT[:, Tg * 128:(Tg + 1) * 128], pb3[:, 0:128])
                  k.ts(V, wi[:, i, :], p_[:, 352:360], (8 ** -0.5) * (64 ** -0.5), None, ALU.mult)
              if sti == 0:
                  dump("dts", dts[:], [128, 4, 32])
                  dump("kvtok", kvtok[:, 0:4, :], [128, 4, 258], BF16)
                  dump("kidxT", kidxT[:, 0:512], [128, 512], BF16)
              chk(3)
              sAb = sA[:]
              gsets = []
              for q_ in range(2):
                  base = sC[:] if q_ == 0 else ybuf[:, 16:24, :].rearrange("p a b -> p (a b)")
                  gsets.append(dict(
                      Xg=base[:, 0:2048].rearrange("p (i c) -> p i c", i=4),
                      Bg=base[:, 2048:2560].rearrange("p (i c) -> p i c", i=4),
                      BT=base[:, 2560:3072], CT=base[:, 3072:3584],
                      xsT=base[:, 3584:4096],
                      sz=sAb[:, q_ * 2048:(q_ + 1) * 2048].rearrange("p (i c) -> p i c", i=4)))

              pjb = [0]

              def pnps():
                  pjb[0] ^= 1
                  return ps[pjb[0]]

              def proj_gen(g):
                  gs = gsets[g % 2]
                  wv = load_w('wxs', g)
                  for j in range(4):
                      p_ = pnps()
                      fm_proj(wv, j, p_)
                      yield
                      conv(p_[:], 4 * g + j, gs['xsT'], hal, cw, cb, 4, AF.Silu)
                      yield
                      p2 = pnps()
                      pb = psb(p2)
                      for i in range(4):
                          k.tr(pb[:, i * 128:(i + 1) * 128], gs['xsT'][:, i * 128:(i + 1) * 128], identb[:])
                      k.copy(A, gs['Xg'][:, :, j * 128:(j + 1) * 128], pb[:, 0:512].rearrange("p (i c) -> p i c", i=4))
                      yield
                  wv = load_w('wbc', g)
                  p_ = pnps()
                  fm_proj(wv, 0, p_)
                  yield
                  conv(p_[:], 16 + g, gs['BT'], hal, cw, cb, 4, AF.Silu)
                  yield
                  p2 = pnps()
                  pb = psb(p2)
                  for i in range(4):
                      k.tr(pb[:, i * 128:(i + 1) * 128], gs['BT'][:, i * 128:(i + 1) * 128], identb[:])
                  k.copy(A, gs['Bg'], pb[:, 0:512].rearrange("p (i c) -> p i c", i=4))
                  yield
                  p_ = pnps()
                  fm_proj(wv, 1, p_)
                  yield
                  conv(p_[:], 20 + g, gs['CT'], hal, cw, cb, 4, AF.Silu)
                  yield
                  wv = load_w('wz', g)
                  for i in range(4):
                      p_ = pnps()
                      for kc in range(8):
                          k.mm(p_[:], hT[:, kc, i * 128:(i + 1) * 128], wv[:, kc, :], start=(kc == 0), stop=(kc == 7))
                      k.act(gs['sz'][:, i, :], p_[:], AF.Silu)
                      yield

              def iter_gen(g, i):
                  gs = gsets[g % 2]
                  Xg_, Bg_, BT_, CT_, szg_ = gs['Xg'], gs['Bg'], gs['BT'], gs['CT'], gs['sz']
                  g8 = slice(8 * g, 8 * g + 8)
                  tc_ = slice(i * 128, (i + 1) * 128)
                  par = i % 2
                  ba, bb_, bc_ = (ps[2], ps[3], ps[4]) if par == 0 else (ps[5], ps[6], ps[7])
                  sv = sst[par]
                  if par == 0:
                      rhsD_, expD_, Gm_, Xdt_, Xd0_, Xd1_ = rhsD, expD, Gm, Xdt, Xd, sD[:, 3584:4096]
                      S0b_, S1b_, vn_, t1, yv = S0b, S1b, vn, fs[2][:, 0:512], fs[3][:, 0:512]
                  else:
                      s16b = s16[:].bitcast(BF16)
                      jb = junk[:].bitcast(BF16)
                      rhsD_, expD_ = s16[:, 0:1024], s16[:, 1024:2048]
                      t1, yv = s16[:, 2048:2560], s16[:, 2560:3072]
                      Gm_, Xdt_, Xd0_ = s16b[:, 6144:7168], s16b[:, 7168:7680], s16b[:, 7680:8192]
                      Xd1_, S0b_, S1b_, vn_ = jb[:, 0:512], jb[:, 512:1024], jb[:, 1024:1536], jb[:, 1536:2048]
                  cbm = cbms[par][:]
                  a8 = sv[:, 0:8]
                  k.tt(V, a8, dts[:, i, g8], Aneg[:, g8], ALU.mult)
                  pE = ba
                  k.mm(pE[:, 0:8], Tm[:], a8)
                  k.mm(pE[:, 8:16], U[:], a8)
                  k.mm(pE[:, 16:24], onesblk[:, 0:128], a8)
                  k.mm(pE[:, 24:32], onesblk[:, 128:256], a8)
                  EX = sv[:, 8:40]
                  k.act(EX, pE[:, 0:32], AF.Exp)
                  yield
                  k.ts(V, sv[:, 40:48], EX[:, 8:16], pmask[:, 0:1], None, ALU.mult)
                  k.ts(V, sv[:, 48:56], EX[:, 8:16], pmask[:, 1:2], None, ALU.mult)
                  rD3 = rhsD_.rearrange("p (h l) -> p h l", h=8)
                  k.tt(V, rD3, bc(a8, 128), Tm[:].unsqueeze(1).to_broadcast([128, 8, 128]), ALU.mult)
                  yield
                  for b in range(2):
                      pD = bb_ if b == 0 else bc_
                      k.mm(pD[:], U[:], rhsD_[:, b * 512:(b + 1) * 512])
                      k.act(expD_[:, b * 512:(b + 1) * 512], pD[:], AF.Exp)
                  yield
                  pC = ba
                  k.mm(pC[:, 0:128], BT_[:, tc_], CT_[:, tc_])
                  k.tt(V, cbm, pC[:, 0:128], Tm[:], ALU.mult)
                  yield
                  k.tt(V, Gm_.rearrange("p (h l) -> p h l", h=8), expD_.rearrange("p (h l) -> p h l", h=8),
                       cbm.unsqueeze(1).to_broadcast([128, 8, 128]), ALU.mult)
                  X3 = Xg_[:, i, :].rearrange("p (h q) -> p h q", h=8)
                  Xdt3 = Xdt_.rearrange("p (h q) -> p h q", h=8)
                  k.tt(G, Xdt3, X3, bc(dts[:, i, g8], 64), ALU.mult)
                  yield
                  k.tt(G, Xd0_.rearrange("p (h q) -> p h q", h=8), Xdt3, bc(sv[:, 40:48], 64), ALU.mult)
                  k.tt(G, Xd1_.rearrange("p (h q) -> p h q", h=8), Xdt3, bc(sv[:, 48:56], 64), ALU.mult)
                  yield 'B'
                  pY = ba
                  for h in range(8):
                      k.mm(pY[:, h * 64:(h + 1) * 64], Gm_[:, h * 128:(h + 1) * 128], Xdt_[:, h * 64:(h + 1) * 64])
                  pS0 = bb_
                  pS1 = bc_
                  k.mm(pS0[:], Bg_[:, i, :], Xd0_)
                  k.mm(pS1[:], Bg_[:, i, :], Xd1_)
                  yield
                  Sg = S[:, g, :]
                  Sg3 = Sg.rearrange("p (h q) -> p h q", h=8)
                  k.copy(A, S0b_, Sg)
                  k.tt(G, Sg3, Sg3, bc(EX[:, 16:24], 64), ALU.mult)
                  k.tt(V, Sg, Sg, pS0[:], ALU.add)
                  yield
                  k.copy(A, S1b_, Sg)
                  k.tt(G, Sg3, Sg3, bc(EX[:, 24:32], 64), ALU.mult)
                  k.tt(V, Sg, Sg, pS1[:], ALU.add)
                  yield
                  pO = bb_
                  k.mm(pO[0:64, :], CT_[:, i * 128:i * 128 + 64], S0b_)
                  k.mm(pO[64:128, :], CT_[:, i * 128 + 64:(i + 1) * 128], S1b_)
                  k.tt(V, t1.rearrange("p (h q) -> p h q", h=8), pO[:].rearrange("p (h q) -> p h q", h=8),
                       bc(EX[:, 0:8], 64), ALU.mult)
                  yield
                  k.tt(V, yv, t1, pY[:], ALU.add)
                  k.tt(G, t1.rearrange("p (h q) -> p h q", h=8), X3, bc(hpb[:, 2, g8], 64), ALU.mult)
                  yield
                  k.tt(G, yv, yv, t1, ALU.add)
                  k.tt(G, yv, yv, szg_[:, i, :], ALU.mult)
                  k.act(t1, yv, AF.Square, accum=sv[:, 56:57])
                  yield
                  rstd_from(sv[:, 56:57], sv[:, 57:58], 1.0 / 512, EPS)
                  k.ts(V, vn_, yv, sv[:, 57:58], None, ALU.mult)
                  yield
                  p2 = bc_
                  pb = psb(p2)
                  for j in range(4):
                      k.tr(pb[:, j * 128:(j + 1) * 128], vn_[:, j * 128:(j + 1) * 128], identb[:])
                  for j in range(4):
                      k.ts(V, ysT[:, 4 * g + j, tc_], pb[:, j * 128:(j + 1) * 128],
                           ng[:, 4 * g + j:4 * g + j + 1], None, ALU.mult)
                  yield

              def run_all(gen):
                  for _ in gen:
                      pass

              def run_until_B(gen):
                  for r in gen:
                      if r == 'B':
                          return

              def interleave(gens):
                  gens = [g_ for g_ in gens if g_ is not None]
                  while gens:
                      alive = []
                      for g_ in gens:
                          try:
                              r = next(g_)
                              alive.append(g_)
                          except StopIteration:
                              pass
                      gens = alive

              def take(gen, n):
                  def sub():
                      for _ in range(n):
                          try:
                              next(gen)
                          except StopIteration:
                              return
                          yield
                  return sub()

              def untilB(gen):
                  def sub():
                      for r in gen:
                          if r == 'B':
                              return
                          yield
                  return sub()

              run_all(proj_gen(0))
              for g in range(4):
                  its = [iter_gen(g, i) for i in range(4)]
                  pj = proj_gen(g + 1) if g < 3 else None
                  run_until_B(its[0])
                  for n in range(4):
                      nxt = untilB(its[n + 1]) if n < 3 else None
                      pjs = take(pj, 6) if pj is not None else None
                      interleave([its[n], nxt, pjs])
                  if pj is not None:
                      run_all(pj)
                  if sti == 0 and g == 0:
                      dump("Xg", gsets[0]['Xg'], [128, 4, 512], BF16)
                      dump("CT", gsets[0]['CT'], [128, 512], BF16)
              if sti == 0:
                  dump("ysT", ysT, [128, 16, 512], BF16)
              chk(4)
              for b in range(2):
                  wv = load_w('wq', b)
                  for m in range(4):
                      p_ = nps()
                      fm_proj(wv, m, p_)
                      k.copy(A, qT[:, b * 4 + m, :], p_[:])
              wv = load_w('wqi', 0)
              for m in range(4):
                  p_ = nps()
                  fm_proj(wv, m, p_)
                  k.copy(A, qiT[:, m, :], p_[:])
              chk(4.1)
              psmod[0] = 4
              k.memset(V, qTm[64:128, 0, :, :], 0.0)
              k.memset(G, qTm[0:64, 1, :, :], 0.0)
              k.memset(V, qiTm[64:128, 0, :, :], 0.0)
              k.memset(G, qiTm[0:64, 1, :, :], 0.0)
              maskTs = [maskT, maskT2[:].rearrange("p (k t) -> p k t", t=128)]

              def idx_a(i):
                  Tg = sti * 4 + i
                  nk = (Tg + 1) * 128
                  tc_ = slice(i * 128, (i + 1) * 128)
                  nb4 = (nk + 511) // 512
                  k.copy(V, qiTm[0:64, 0, :, :], qiT[0:64, :, tc_])
                  k.copy(G, qiTm[64:128, 1, :, :], qiT[64:128, :, tc_])
                  for kb4 in range(nb4):
                      c0 = kb4 * 512
                      ncol = min(512, nk - c0)
                      for h in range(8):
                          p_ = nps()
                          k.mm(p_[:, 0:ncol], qiTm[:, h % 2, h // 2, :], kidxT[:, c0:c0 + ncol])
                          rl = fs[h % 2][:, 0:ncol]
                          k.act(rl, p_[:, 0:ncol], AF.Relu)
                          if h == 0:
                              k.ts(V, score[:, c0:c0 + ncol], rl, wi[:, i, 0:1], None, ALU.mult)
                          else:
                              k.stt(score[:, c0:c0 + ncol], rl, wi[:, i, h:h + 1], score[:, c0:c0 + ncol],
                                    ALU.mult, ALU.add)
                  thr = bst[:, 2:3]
                  if nk > 256:
                      Bm = bst[:, 4:5]
                      P.add(V, lambda e: e.tensor_reduce(Bm, score[:, 0:nk], AX.X, ALU.max, apply_absolute_value=True),
                            reads=[score[:, 0:nk]], writes=[Bm])
                      k.ts(V, Bm, Bm, 1e-20, 1.0000001, ALU.add, ALU.mult)
                      hwt = bst[:, 8:8 + NBIS + 1]
                      k.ts(V, hwt, pw2[:, 0:NBIS + 1], Bm, None, ALU.mult)
                      hwn = bst[:, 32:32 + NBIS + 1]
                      k.ts(V, hwn, hwt, -0.5, None, ALU.mult)
                  k.tt(V, score[:, Tg * 128:(Tg + 1) * 128], score[:, Tg * 128:(Tg + 1) * 128], cmneg[:], ALU.add)
                  if nk > 256:
                      k.memset(V, bst[:, 0:1], 0.0)
                  else:
                      k.memset(V, thr, -10000.0)

              def bis_part(i, it0, it1):
                  Tg = sti * 4 + i
                  nk = (Tg + 1) * 128
                  if nk <= 256:
                      return
                  mid = bst[:, 0:1]
                  cnt = bst[:, 1:2]
                  dd = bst[:, 3:4]
                  hwt = bst[:, 8:8 + NBIS + 1]
                  hwn = bst[:, 32:32 + NBIS + 1]
                  for it in range(it0, it1):
                      k.ts(V, junk[:, 0:nk], score[:, 0:nk], mid, None, ALU.is_ge, ALU.add, accum=cnt)
                      k.ts(V, dd, cnt, 255.5, hwt[:, it:it + 1], ALU.is_ge, ALU.mult)
                      k.stt(mid, dd, hwn[:, it:it + 1], mid, ALU.add, ALU.add)
                  if it1 == NBIS:
                      k.tt(V, bst[:, 2:3], mid, hwt[:, NBIS:NBIS + 1], ALU.subtract)

              def idx_b(i):
                  Tg = sti * 4 + i
                  nk = (Tg + 1) * 128
                  nb4 = (nk + 511) // 512
                  mT_ = maskTs[i % 2]
                  thr = bst[:, 2:3]
                  for kb4 in range(nb4):
                      c0 = kb4 * 512
                      ncol = min(512, nk - c0)
                      mb_ = mblk if kb4 % 2 == 0 else sD[:, 3584:4096]
                      k.ts(V, mb_[:, 0:ncol], score[:, c0:c0 + ncol], thr, None, ALU.is_ge)
                      p2 = nps()
                      pb = psb(p2)
                      for q in range(ncol // 128):
                          k.tr(pb[:, q * 128:(q + 1) * 128], mb_[:, q * 128:(q + 1) * 128], identb[:])
                      k.copy(A, mT_[:, kb4 * 4:kb4 * 4 + ncol // 128, :],
                             pb[:, 0:ncol].rearrange("p (q t) -> p q t", t=128))

              bias8 = biasT
              PT3s = [sD[:, 2048:2560], sD[:, 2560:3072], sD[:, 3072:3584]]
              Lb = [ps[1], ps[2], ps[3]]
              olb = hb[:, 0:256]
              oTb = hb[:, 256:512].rearrange("p (c t) -> p c t", c=2)

              def main_stage(i):
                  Tg = sti * 4 + i
                  tc_ = slice(i * 128, (i + 1) * 128)
                  mT_ = maskTs[i % 2]
                  k.copy(V, qTm[0:64, 0, :, :], qT[0:64, :, tc_])
                  k.copy(G, qTm[64:128, 1, :, :], qT[64:128, :, tc_])
                  for hg in range(4):
                      for cc in range(2):
                          p_ = ps[0]
                          for hh in range(4):
                              h = 4 * hg + hh
                              k.mm(p_[:, hh * 128:(hh + 1) * 128], wukT[:, h // 2, cc * 128:(cc + 1) * 128],
                                   qTm[:, h % 2, h // 2, :])
                          k.copy(A, qlT[:, cc, 4 * hg:4 * hg + 4, :], p_[:].rearrange("p (h t) -> p h t", h=4))
                  tot = 4 * (Tg + 1)
                  done = [0, 0]

                  def tick():
                      done[0] += 1
                      if i < 3:
                          want = min(NBIS, (done[0] * NBIS + tot - 1) // tot)
                          if want > done[1]:
                              bis_part(i + 1, done[1], want)
                              done[1] = want

                  for hgx in range(4):
                      its = [(hgx, kb) for kb in range(Tg + 1)]
                      run_its(its, Tg, tc_, mT_, tick)

              def run_its(its, Tg, tc_, mT_, tick):
                  def qk(n):
                      hg, kb = its[n]
                      pL = Lb[n % 3]
                      near = kb >= Tg - 1
                      for cc in range(2):
                          k.mm(pL[:], kvnT[:, cc, kb * 128:(kb + 1) * 128], qlT[:, cc, 4 * hg:4 * hg + 4, :],
                               start=(cc == 0), stop=(cc == 1 and not near))
                      if near:
                          off = 128 if kb == Tg else 0
                          k.mm(pL[:], identb[:], bias8[:, 4 * hg:4 * hg + 4, off:off + 128], start=False, stop=True)

                  def softmax_part(n):
                      hg, kb = its[n]
                      pL = Lb[n % 3]
                      PT = PT3s[n % 3]
                      k.act(PT, pL[:], AF.Exp, scale=0.125)
                      PT3 = PT.rearrange("p (h t) -> p h t", h=4)
                      k.tt(G if n % 2 == 0 else V, PT3, PT3, mT_[:, kb, :].unsqueeze(1).to_broadcast([128, 4, 128]), ALU.mult)

                  def pv(n):
                      hg, kb = its[n]
                      PT = PT3s[n % 3]
                      for hh in range(4):
                          k.mm(ps[4 + hh][:, 0:257], PT[:, hh * 128:(hh + 1) * 128], kvtok[:, kb, 0:257],
                               start=(kb == 0), stop=(kb == Tg))
                      if kb == Tg:
                          for hh in range(4):
                              h = 4 * hg + hh
                              po = 64 * (h % 2)
                              acc = ps[4 + hh]
                              rv = bst[:, 60 + hh:61 + hh]
                              k.recip(rv, acc[:, 256:257])
                              k.ts(V, olb, acc[:, 0:256], rv, None, ALU.mult)
                              pb = psb(ps[0])
                              for cc in range(2):
                                  k.tr(pb[:, cc * 128:(cc + 1) * 128], olb[:, cc * 128:(cc + 1) * 128], identb[:])
                              k.copy(A, oTb, pb[:, 0:256].rearrange("p (c t) -> p c t", c=2))
                              pYa = ps[0]
                              for cc in range(2):
                                  k.mm(pYa[po:po + 64, 256:384], wuv[:, cc, h * 64:(h + 1) * 64], oTb[:, cc, :],
                                       start=(cc == 0), stop=(cc == 1))
                              if h % 2 == 1:
                                  k.copy(A, yaT[:, h // 2, tc_], pYa[:, 256:384])

                  N = len(its)
                  LA = 2
                  for n in range(min(LA, N)):
                      qk(n)
                  for n in range(N):
                      softmax_part(n)
                      if n + LA < N:
                          qk(n + LA)
                      pv(n)
                      tick()

              idx_a(0)
              bis_part(0, 0, NBIS)
              idx_b(0)
              for i in range(4):
                  if i < 3:
                      idx_a(i + 1)
                  main_stage(i)
                  if i < 3:
                      idx_b(i + 1)
              psmod[0] = 8
              if sti == 0:
                  dump("yaT", yaT, [128, 8, 512], BF16)
              chk(5)
              for q4 in range(2):
                  wv = load_w('wgs', q4)
                  for m in range(4):
                      p_ = nps()
                      fm_proj(wv, m, p_)
                      k.act(sgs[:, m, :], p_[:], AF.Sigmoid)
                  wv = load_w('wga', q4)
                  for m in range(4):
                      p_ = nps()
                      fm_proj(wv, m, p_)
                      k.act(sga[:, m, :], p_[:], AF.Sigmoid)
                  for b2 in range(2):
                      wv = load_w('wbs', 2 * q4 + b2)
                      for m in range(2):
                          p_ = nps()
                          for kc in range(16):
                              k.mm(p_[:], wv[:, kc, m * 128:(m + 1) * 128], ysT[:, kc, :], start=(kc == 0), stop=(kc == 15))
                          k.tt(V, mT[:, 4 * q4 + 2 * b2 + m, :], p_[:], sgs[:, 2 * b2 + m, :], ALU.mult)
                  wv = load_w('wba', q4)
                  for m in range(4):
                      p_ = nps()
                      for kc in range(8):
                          k.mm(p_[:], wv[:, kc, m * 128:(m + 1) * 128], yaT[:, kc, :], start=(kc == 0), stop=(kc == 7))
                      k.tt(V, fs[2][:, 0:512], p_[:], sga[:, m, :], ALU.mult)
                      k.tt(V, mT[:, 4 * q4 + m, :], mT[:, 4 * q4 + m, :], fs[2][:, 0:512], ALU.add)
              if sti == 0:
                  dump("mT", mT, [128, 8, 512], BF16)
              wo = [load_w('wo', 0), load_w('wo', 1)]
              load_ln('ln0')
              for i in range(4):
                  Tg = sti * 4 + i
                  P.dma('sync', xt[:], x_d[Tg * 128:(Tg + 1) * 128, :])
                  layernorm(xt[:], xt[:])
                  for b2 in range(2):
                      p_ = nps()
                      for kc in range(8):
                          k.mm(p_[:], mT[:, kc, i * 128:(i + 1) * 128], wo[b2][:, kc, :], start=(kc == 0), stop=(kc == 7))
                      k.stt(h1[:, i, b2 * 512:(b2 + 1) * 512], xt[:, b2 * 512:(b2 + 1) * 512], ALPHA, p_[:],
                            ALU.mult, ALU.add)
              load_ln('ln1')
              for i in range(4):
                  layernorm(h1[:, i, :], h1[:, i, :], hb[:])
                  to_hT(i)
              if sti == 0:
                  dump("h1", h1, [128, 4, 1024])
              chk(6)
              for blk in range(11):
                  wv = load_w('wup', blk)
                  for jj in range(2):
                      j = 2 * blk + jj
                      p_ = nps()
                      fm_proj(wv, 2 * jj, p_)
                      ga = fs[2][:, 0:512]
                      conv(p_[:], j, ga, fhal, fcw, fcb, 3, AF.Silu)
                      p2 = nps()
                      fm_proj(wv, 2 * jj + 1, p2)
                      gv = fs[3][:, 0:512]
                      conv(p2[:], 22 + j, gv, fhal, fcw, fcb, 3, None)
                      k.tt(V, gT[:, j, :], ga, gv, ALU.mult)
              for nb in range(8):
                  wv = load_w('wdn', nb)
                  p_ = nps()
                  for i in range(4):
                      for j in range(22):
                          k.mm(p_[:, i * 128:(i + 1) * 128], gT[:, j, i * 128:(i + 1) * 128], wv[:, j, :],
                               start=(j == 0), stop=(j == 21))
                  for i in range(4):
                      k.stt(h1[:, i, nb * 128:(nb + 1) * 128], h1[:, i, nb * 128:(nb + 1) * 128], ALPHA,
                            p_[:, i * 128:(i + 1) * 128], ALU.mult, ALU.add)
              load_ln('ln2')
              for i in range(4):
                  Tg = sti * 4 + i
                  layernorm(h1[:, i, :], h1[:, i, :])
                  P.dma('sync', out_d[Tg * 128:(Tg + 1) * 128, :], h1[:, i, :])
        except _Stop:
            pass
        P.emit()
    return nc


def make_inmaps(inputs, L, nb):
    cw_ = host_consts()
    ww = host_weights(inputs)
    ss = host_small(inputs)
    base = {}
    for kk, v in ww.items():
        base[kk] = np.ascontiguousarray(v.reshape(v.shape[0] * 128, -1))
    base.update(ss)
    base.update(cw_)
    x = np.asarray(inputs['x'], np.float32)
    maps = []
    for c in range(nb):
        m = dict(base)
        m['x'] = np.ascontiguousarray(x[c % x.shape[0], :L])
        maps.append(m)
    return maps


def kernel(**inputs):
    L = SEQ
    nc = build(L)
    maps = make_inmaps(inputs, L, 8)
    res = run_bass_kernel_spmd(nc, maps, core_ids=list(range(8)))
    out = np.stack([np.asarray(res.results[c]["out"], np.float32) for c in range(4)], axis=0)
    return out
```

```python
import math
from contextlib import ExitStack
import numpy as np
import concourse.bass as bass
import concourse.mybir as mybir
from concourse.bass_utils import run_bass_kernel_spmd

F32 = mybir.dt.float32
BF16 = mybir.dt.bfloat16
U8 = mybir.dt.uint8
ALU = mybir.AluOpType
AF = mybir.ActivationFunctionType
AX = mybir.AxisListType

ENGS = ['sync', 'scalar', 'vector', 'gpsimd', 'tensor']
NDS = 8


def _acc(x):
    if isinstance(x, (str, tuple)):
        return (x, 0, 1 << 30, 0, 1 << 60)
    name = x.name
    if not hasattr(x, 'offset') or not hasattr(x, 'ap'):
        return (name, 0, 1 << 30, 0, 1 << 60)
    sz = mybir.dt.size(x.dtype)
    ap = x.ap
    off = x.offset
    if name.startswith('ps'):
        return (name, 0, 1 << 30, 0, 1 << 60)
    if name.startswith('sb_'):
        pitch = ap[0][0]
        if pitch <= 0:
            return (name, 0, 1 << 30, 0, 1 << 60)
        plo = off // pitch
        phi = plo + ap[0][1]
        lo = off % pitch
        hi = lo + sum((c - 1) * abs(st_) for st_, c in ap[1:]) + 1
        return (name, plo, phi, lo * sz, hi * sz)
    hi = off + sum((c - 1) * abs(st_) for st_, c in ap) + 1
    return (name, 0, 1, off * sz, hi * sz)


def _ovl(a, b):
    return a[1] < b[2] and b[1] < a[2] and a[3] < b[4] and b[3] < a[4]


def _cov(a, b):
    return a[1] <= b[1] and b[2] <= a[2] and a[3] <= b[3] and b[4] <= a[4]


class Prog:
    def __init__(self, nc, es):
        self.nc = nc
        self.es = es
        self.ol = []
        self.W = {}
        self.R = {}
        self.sems = {}

    def sem(self, k):
        if k not in self.sems:
            self.sems[k] = self.es.enter_context(self.nc.semaphore("s_" + "_".join(str(a) for a in k)))
        return self.sems[k]

    def add(self, eng, fn, reads=(), writes=(), dma=False, cost=0.3):
        oid = len(self.ol)
        deps = {}
        ra = [_acc(k) for k in reads]
        wa = [_acc(k) for k in writes]
        for a in ra:
            for w in self.W.get(a[0], ()):
                if _ovl(w, a):
                    deps[w[5]] = True
        for a in wa:
            for lst in (self.W.get(a[0], ()), self.R.get(a[0], ())):
                for w in lst:
                    if _ovl(w, a) and w[5] != oid:
                        d = self.ol[w[5]]
                        pe_pe = (not dma) and (not d[2]) and d[0] == eng == 'tensor'
                        if w[5] not in deps:
                            deps[w[5]] = not pe_pe
        pin = OPT.get('pin', ('scalar',))
        if (eng in pin and not dma) or (dma and ('dma_' + eng) in pin):
            pk = ('dma_' + eng) if dma else eng
            lp = getattr(self, '_last', {}).get(pk)
            if lp is not None:
                deps.setdefault(lp, False)
            if not hasattr(self, '_last'):
                self._last = {}
            self._last[pk] = oid
        self.ol.append([eng, fn, dma, cost, deps])
        for a in wa:
            e = a + (oid,)
            self.W[a[0]] = [w for w in self.W.get(a[0], ()) if not _cov(a, w)] + [e]
            self.R[a[0]] = [r for r in self.R.get(a[0], ()) if not _cov(a, r)]
        for a in ra:
            e = a + (oid,)
            lst = self.R.get(a[0], [])
            if not dma:
                keep = []
                for r in lst:
                    if self.ol[r[5]][0] == eng and not self.ol[r[5]][2] and _cov(a, r) and r[5] != oid:
                        self.ol[oid][4].setdefault(r[5], False)
                    else:
                        keep.append(r)
                lst = keep
            lst.append(e)
            self.R[a[0]] = lst
        return oid

    def dma(self, eng, out, in_, reads=None, writes=None, **kw):
        nb = 0
        try:
            nb = out.nbytes() if callable(getattr(out, 'nbytes', None)) else out.nbytes
        except Exception:
            nb = 1 << 16
        return self.add(eng, lambda e: e.dma_start(out=out, in_=in_, **kw),
                        reads=[in_] if reads is None else reads,
                        writes=[out] if writes is None else writes, dma=True, cost=2.0 + nb / 150e3)

    def schedule(self):
        ol = self.ol
        n = len(ol)
        if OPT.get('nosched'):
            order = {e: [] for e in ENGS}
            for i, o in enumerate(ol):
                order[o[0]].append(i)
            return order
        succ = [[] for _ in range(n)]
        ndep = [0] * n
        for i, o in enumerate(ol):
            ndep[i] = len(o[4])
            for d in o[4]:
                succ[d].append(i)
        fin = [0.0] * n
        start_ok = [0.0] * n
        eng_free = {e: 0.0 for e in ENGS}
        ready = {e: [] for e in ENGS}
        import bisect
        for i in range(n):
            if ndep[i] == 0:
                ready[ol[i][0]].append(i)
        order = {e: [] for e in ENGS}
        done = 0
        lo = 0
        sched = [False] * n
        WIN = 700
        NC = 6
        while done < n:
            best = None
            while lo < n and sched[lo]:
                lo += 1
            for e in ENGS:
                r = ready[e]
                c = 0
                for i in r:
                    if i >= lo + WIN:
                        break
                    est = max(eng_free[e], start_ok[i])
                    key = (est, i)
                    if best is None or key < best[0]:
                        best = (key, e, i)
                    c += 1
                    if c >= NC:
                        break
            if best is None:
                cand = [(ready[e][0], e) for e in ENGS if ready[e]]
                i, e = min(cand)
                best = ((max(eng_free[e], start_ok[i]), i), e, i)
            (est, _), e, i = best
            o = ol[i]
            ready[e].remove(i)
            sched[i] = True
            done += 1
            order[e].append(i)
            if o[2]:
                eng_free[e] = est + 0.15
                fin[i] = est + o[3]
            else:
                eng_free[e] = est + o[3]
                fin[i] = est + o[3]
            for sidx in succ[i]:
                lat = 0.05 if (ol[sidx][0] == e and not o[2]) else 0.3
                t = fin[i] + lat
                if t > start_ok[sidx]:
                    start_ok[sidx] = t
                ndep[sidx] -= 1
                if ndep[sidx] == 0:
                    bisect.insort(ready[ol[sidx][0]], sidx)
        return order

    def emit(self):
        nc = self.nc
        ol = self.ol
        order = self.schedule()
        tok = [None] * len(ol)
        cnt = {e: 0 for e in ENGS}
        dman = {e: 0 for e in ENGS}
        dmal = {e: [] for e in ENGS}
        plan = {e: [] for e in ENGS}
        for e in ENGS:
            for i in order[e]:
                o = ol[i]
                if o[2]:
                    k_ = dman[e]
                    dman[e] += 1
                    tok[i] = (('d', e, k_ % NDS), 16 * (k_ // NDS + 1), e, True)
                    if k_ >= NDS:
                        o[4][dmal[e][k_ - NDS]] = True
                    dmal[e].append(i)
                else:
                    cnt[e] += 1
                    if cnt[e] % 60000 == 0:
                        cnt[e] += 1
                    tok[i] = (('c', e, cnt[e] // 60000), cnt[e] % 60000, e, False)
        fin = {}
        for e in ENGS:
            waited = {}
            for i in order[e]:
                o = ol[i]
                waits = {}
                for d, need in o[4].items():
                    if not need:
                        continue
                    s_, v, _, _ = tok[d]
                    if waited.get(s_, 0) >= v:
                        continue
                    waits[s_] = max(waits.get(s_, 0), v)
                for s_, v in waits.items():
                    waited[s_] = v
                plan[e].append((list(waits.items()), o[1], tok[i]))
                fin[tok[i][0]] = max(fin.get(tok[i][0], 0), tok[i][1])
        for e in ENGS:
            for (ws, _, t) in plan[e]:
                self.sem(t[0])
                for a, _ in ws:
                    self.sem(a)
        with nc.Block() as block:
            def run(engname):
                def body(e):
                    for (waits, fn, t) in plan[engname]:
                        for s_, v in waits:
                            e.wait_ge(self.sem(s_), v)
                        ins = fn(e)
                        ins.then_inc(self.sem(t[0]), 16 if t[3] else 1)
                    if engname == 'sync':
                        for s_, v in fin.items():
                            e.wait_ge(self.sem(s_), v)
                return body
            block.sync(run('sync'))
            block.scalar(run('scalar'))
            block.vector(run('vector'))
            block.gpsimd(run('gpsimd'))
            block.tensor(run('tensor'))


OPT = {}
D_MODEL = 1024
SEQ = 4096
N_IN = 9064
D_FF = 2816
ALPHA = 2.0 ** 0.25
EPS = 1e-5
NEG = -30000.0
NBIS = 20
CHKT = 2
BIS_HW = 256.0

O_Z, O_XBC, O_DT, O_Q, O_CKV, O_QI, O_KI, O_WI, O_GS, O_GA = 0, 2048, 5120, 5152, 6176, 6432, 6944, 7008, 7016, 8040


def _t5_bucket(rel):
    nb = 16
    max_exact = 8
    bucket = np.where(rel > 0, nb, 0)
    n = np.abs(rel)
    nf = np.maximum(n, 1).astype(np.float32)
    large = max_exact + (np.log(nf / max_exact) / math.log(128 / max_exact) * (nb - max_exact)).astype(np.int32)
    large = np.minimum(large, nb - 1)
    return bucket + np.where(n < max_exact, n, large)


def _blk(W, cols, N):
    K = W.shape[0]
    cols = np.asarray(cols)
    nb = len(cols) // N
    Wc = W[:, cols].reshape(K // 128, 128, nb, N)
    return np.ascontiguousarray(Wc.transpose(2, 1, 0, 3)).astype(np.float32)


def host_consts():
    c = {}
    p = np.arange(128)
    same = (p[:, None] // 64) == (p[None, :] // 64)
    c['Tm'] = (same & (p[:, None] <= p[None, :])).astype(np.float32)
    c['U'] = (same & (p[:, None] > p[None, :])).astype(np.float32)
    ob = np.zeros((128, 256), np.float32)
    ob[:64, :128] = 1.0
    ob[64:, 128:] = 1.0
    c['onesblk'] = ob
    c['cmneg'] = np.where((p[None, :] // 64) <= (p[:, None] // 64), 0.0, NEG).astype(np.float32)
    c['identf'] = np.eye(128, dtype=np.float32)
    j = np.arange(256)
    oh = np.zeros((2, 32, 256), np.float32)
    for k, rel in enumerate([-1 - j, 127 - j]):
        b = _t5_bucket(rel)
        oh[k, b, j] = 1.0
        oh[k, 15, :] -= 1.0
    c['ohb'] = oh
    c['ones32'] = np.ones((32, 128), np.float32)
    return c


def host_weights(inp):
    w = {}
    w_in = np.asarray(inp['w_in'][0], np.float32)
    w['wsm'] = _blk(w_in, list(range(O_DT, O_DT + 32)) + list(range(O_CKV, O_CKV + 256)) +
                    list(range(O_KI, O_KI + 64)) + list(range(O_WI, O_WI + 8)), 360)
    w['wz'] = _blk(w_in, range(O_Z, O_Z + 2048), 512)
    w['wxs'] = _blk(w_in, range(O_XBC, O_XBC + 2048), 512)
    bc = []
    for g in range(4):
        bc += list(range(O_XBC + 2048 + g * 128, O_XBC + 2048 + (g + 1) * 128))
        bc += list(range(O_XBC + 2560 + g * 128, O_XBC + 2560 + (g + 1) * 128))
    w['wbc'] = _blk(w_in, bc, 256)
    w['wq'] = _blk(w_in, range(O_Q, O_Q + 1024), 512)
    w['wqi'] = _blk(w_in, range(O_QI, O_QI + 512), 512)
    w['wgs'] = _blk(w_in, range(O_GS, O_GS + 1024), 512)
    w['wga'] = _blk(w_in, range(O_GA, O_GA + 1024), 512)
    w['wbs'] = _blk(np.asarray(inp['w_br_ssd'][0], np.float32), range(1024), 256)
    w['wba'] = _blk(np.asarray(inp['w_br_att'][0], np.float32), range(1024), 512)
    w['wo'] = _blk(np.asarray(inp['w_out'][0], np.float32), range(1024), 512)
    up = []
    for j in range(22):
        up += list(range(j * 128, (j + 1) * 128)) + list(range(D_FF + j * 128, D_FF + (j + 1) * 128))
    w['wup'] = _blk(np.asarray(inp['ffn_w_up'][0], np.float32), up, 512)
    w['wdn'] = _blk(np.asarray(inp['ffn_w_down'][0], np.float32), range(1024), 128)
    return w


W_SHAPES = {'wsm': (1, 8, 360), 'wz': (4, 8, 512), 'wxs': (4, 8, 512), 'wbc': (4, 8, 256), 'wq': (2, 8, 512),
            'wqi': (1, 8, 512), 'wgs': (2, 8, 512), 'wga': (2, 8, 512), 'wbs': (4, 16, 256), 'wba': (2, 8, 512),
            'wo': (2, 8, 512), 'wup': (11, 8, 512), 'wdn': (8, 22, 128)}


def host_small(inp):
    s = {}
    f = lambda a: np.ascontiguousarray(np.asarray(a, np.float32))
    s['ln0'] = f(np.stack([inp['ln_in_g'], inp['ln_in_b']])[None])
    s['ln1'] = f(np.stack([inp['ln1_g'][0], inp['ln1_b'][0]])[None])
    s['ln2'] = f(np.stack([inp['ln2_g'][0], inp['ln2_b'][0]])[None])
    cw = np.asarray(inp['ssd_conv_w'][0], np.float32)
    s['cw'] = f(cw.T.reshape(24, 128, 4).transpose(1, 0, 2))
    s['cb'] = f(np.asarray(inp['ssd_conv_b'][0]).reshape(24, 128).T)
    fw = np.asarray(inp['ffn_conv_w'][0], np.float32)
    s['fcw'] = f(fw.T.reshape(44, 128, 3).transpose(1, 0, 2))
    s['fcb'] = f(np.asarray(inp['ffn_conv_b'][0]).reshape(44, 128).T)
    s['hp'] = f(np.stack([inp['ssd_dt_bias'][0], inp['ssd_A_log'][0], inp['ssd_D'][0]])[None])
    s['ng'] = f(np.asarray(inp['ssd_norm_g'][0]).reshape(16, 128).T)
    s['kvg'] = f(np.asarray(inp['att_kv_norm_g'][0])[None])
    s['ikp'] = f(np.stack([inp['idx_k_norm_g'][0], inp['idx_k_norm_b'][0]])[None])
    uk = np.asarray(inp['att_w_uk'][0], np.float32)
    s['wukT'] = f(uk.transpose(0, 2, 1).reshape(8, 128, 256).transpose(1, 0, 2))
    uv = np.asarray(inp['att_w_uv'][0], np.float32)
    s['wuv'] = f(uv.transpose(1, 0, 2).reshape(2, 128, 1024).transpose(1, 0, 2))
    s['relb'] = f(inp['rel_bias'])
    return s


S_SHAPES = {'ln0': (1, 2, 1024), 'ln1': (1, 2, 1024), 'ln2': (1, 2, 1024), 'cw': (128, 24, 4), 'cb': (128, 24),
            'fcw': (128, 44, 3), 'fcb': (128, 44), 'hp': (1, 3, 32), 'ng': (128, 16), 'kvg': (1, 256),
            'ikp': (1, 2, 64), 'wukT': (128, 8, 256), 'wuv': (128, 2, 1024), 'relb': (32, 16)}
C_SHAPES = {'Tm': (128, 128), 'U': (128, 128), 'onesblk': (128, 256), 'cmneg': (128, 128), 'identf': (128, 128),
            'ohb': (2, 32, 256), 'ones32': (32, 128)}


def _fs(ap):
    try:
        v = ap.free_size
        return v() if callable(v) else v
    except Exception:
        return 512


class K:
    def __init__(self, P):
        self.P = P

    @staticmethod
    def _aps(xs):
        return [x for x in xs if hasattr(x, 'name') and not isinstance(x, (str, float, int))]

    def mm(self, out, lhsT, rhs, start=True, stop=True):
        c_ = 0.06 + _fs(rhs) * 0.00045 * (4 if rhs.dtype == F32 else 1)
        self.P.add('tensor', lambda e: e.matmul(out, lhsT, rhs, start=start, stop=stop),
                   reads=[lhsT, rhs], writes=[out], cost=c_)

    def tr(self, out, in_, ident):
        self.P.add('tensor', lambda e: e.transpose(out, in_, ident), reads=[in_, ident], writes=[out], cost=0.1)

    def act(self, out, in_, func, bias=0.0, scale=1.0, accum=None):
        kw = {}
        if accum is not None:
            kw['accum_out'] = accum
        self.P.add('scalar', lambda e: e.activation(out, in_, func, bias=bias, scale=scale, **kw),
                   reads=self._aps([in_, bias, scale]), writes=self._aps([out, accum]), cost=0.2 + _fs(out) * 0.0009)

    def ts(self, eng, out, in0, s1, s2, op0, op1=None, accum=None):
        kw = {}
        if accum is not None:
            kw['accum_out'] = accum
        if op1 is None:
            op1 = ALU.bypass
        self.P.add(eng, lambda e: e.tensor_scalar(out, in0, s1, s2, op0, op1, **kw),
                   reads=self._aps([in0, s1, s2]), writes=self._aps([out, accum]),
                   cost=(0.15 + _fs(out) * 0.0011) * (2 if eng == 'gpsimd' else 1))

    def tt(self, eng, out, in0, in1, op):
        self.P.add(eng, lambda e: e.tensor_tensor(out, in0, in1, op), reads=[in0, in1], writes=[out],
                   cost=(0.15 + _fs(out) * 0.0011) * (2 if eng == 'gpsimd' else 1))

    def stt(self, out, in0, scalar, in1, op0, op1):
        self.P.add('vector', lambda e: e.scalar_tensor_tensor(out, in0, scalar, in1, op0, op1),
                   reads=self._aps([in0, scalar, in1]), writes=[out], cost=0.15 + _fs(out) * 0.0011)

    def copy(self, eng, out, in_):
        if eng == 'scalar':
            self.P.add('scalar', lambda e: e.copy(out, in_), reads=[in_], writes=[out], cost=0.2 + _fs(out) * 0.0009)
        else:
            self.P.add(eng, lambda e: e.tensor_copy(out, in_), reads=[in_], writes=[out],
                       cost=(0.1 + _fs(out) * 0.0006) * (2 if eng == 'gpsimd' else 1))

    def memset(self, eng, ap, v):
        self.P.add(eng, lambda e: e.memset(ap, v), writes=[ap])

    def recip(self, out, in_):
        self.P.add('vector', lambda e: e.reciprocal(out, in_), reads=[in_], writes=[out])

    def bn_stats(self, out, in_):
        self.P.add('vector', lambda e: e.bn_stats(out, in_), reads=[in_], writes=[out])

    def bn_aggr(self, out, in_):
        self.P.add('vector', lambda e: e.bn_aggr(out, in_), reads=[in_], writes=[out])


class _Stop(Exception):
    pass


def build(L, dbg=(), limit=99):
    NT = L // 128
    NST = L // 512
    nc = bass.Bass("TRN2", target_bir_lowering=False)

    def din(name, shape):
        return nc.dram_tensor(name, list(shape), F32, kind="ExternalInput").ap()

    x_d = din("x", [L, 1024])
    wd = {k: din(k, [v[0] * 128, v[1] * v[2]]) for k, v in W_SHAPES.items()}
    wbf = {k: nc.dram_tensor(k + "_bf", [v[0] * 128, v[1] * v[2]], BF16, kind="Internal").ap()
           for k, v in W_SHAPES.items()}
    sd = {k: din(k, v) for k, v in S_SHAPES.items()}
    cd = {k: din(k, v) for k, v in C_SHAPES.items()}
    out_d = nc.dram_tensor("out", [L, 1024], F32, kind="ExternalOutput").ap()
    zd_t = nc.dram_tensor("zscr", [2, 128, 4096], F32, kind="Internal")
    zd = zd_t.ap()
    dbg_d = {}

    es = ExitStack()
    with es:
        def T(name, shape, dt):
            return es.enter_context(nc.sbuf_tensor("sb_" + name, list(shape), dt))
        P = Prog(nc, es)
        k = K(P)
        V, G, A = 'vector', 'gpsimd', 'scalar'

        identb = T("identb", [128, 128], BF16)
        Tm = T("Tm", [128, 128], F32)
        U = T("U", [128, 128], F32)
        onesblk = T("onesblk", [128, 256], F32)
        cmneg = T("cmneg", [128, 128], F32)
        lnp = T("lnp", [128, 2, 1024], F32)
        cw = T("cw", [128, 24, 4], F32)
        cb = T("cb", [128, 24], F32)
        fcw = T("fcw", [128, 44, 3], F32)
        fcb = T("fcb", [128, 44], F32)
        hpb = T("hpb", [128, 3, 32], F32)
        Aneg = T("Aneg", [128, 32], F32)
        ng = T("ng", [128, 16], F32)
        kvg = T("kvg", [128, 256], F32)
        ikp = T("ikp", [128, 2, 64], F32)
        wukT = T("wukT", [128, 8, 256], BF16)
        wuv = T("wuv", [128, 2, 1024], BF16)
        biasT = T("biasT", [128, 16, 256], BF16)
        kvnT = T("kvnT", [128, 2, L], BF16)
        kvtok = T("kvtok", [128, NT, 258], BF16)
        kidxT = T("kidxT", [128, L], BF16)
        S = T("S", [128, 4, 512], F32)
        hal = T("hal", [128, 24, 3], F32)
        fhal = T("fhal", [128, 44, 2], F32)
        xt = T("xt", [128, 1024], F32)
        hb = T("hb", [128, 1024], BF16)
        hT = T("hT", [128, 8, 512], BF16)
        wb = [T("wb0", [128, 4096], BF16), T("wb1", [128, 4096], BF16)]
        s16 = T("s16", [128, 4096], F32)
        ybuf = T("ybuf", [128, 24, 512], BF16)
        sA = T("sA", [128, 4096], BF16)
        sB = T("sB", [128, 4096], BF16)
        sC = T("sC", [128, 4096], BF16)
        sD = T("sD", [128, 4096], BF16)
        fs = [T("fs%d" % i, [128, 516], F32) for i in range(4)]
        junk = T("junk", [128, max(L, 4096)], U8)
        st = T("st", [128, 64], F32)
        dts = T("dts", [128, 4, 32], F32)
        wi = T("wi", [128, 4, 8], F32)
        qTm = xt[:].bitcast(BF16).rearrange("p (a k t) -> p a k t", a=2, k=8)
        qiTm = lnp[:].rearrange("p a b -> p (a b)").bitcast(BF16)[:, 0:1024].rearrange("p (a k t) -> p a k t", a=2, k=4)
        maskT2 = T("maskT2", [128, 4096], BF16)
        _m2f = maskT2[:].bitcast(F32)
        fs.append(_m2f[:, 0:516])
        fs.append(_m2f[:, 516:1032])
        sst = [T("sst%d" % i, [128, 64], F32) for i in range(2)]
        cbms = [T("cbm%d" % i, [128, 128], F32) for i in range(2)]
        pmask = T("pmask", [128, 2], F32)
        pw2 = T("pw2", [128, 64], F32)
        bst = T("bst", [128, 64], F32)
        ps = [es.enter_context(nc.psum_tensor("ps%d" % i, [128, 512], F32)) for i in range(8)]
        psn = [0]

        psmod = [8]

        def nps():
            psn[0] = (psn[0] + 1) % psmod[0]
            return ps[psn[0]]

        def psb(p):
            return p[:].bitcast(BF16)

        def dump(name, ap, shape, dt=F32):
            if name in dbg:
                d = nc.dram_tensor("dbg_" + name, list(shape), dt, kind="ExternalOutput").ap()
                P.dma('sync', d, ap)

        wbn = [0]

        def load_w(name, blk, eng='sync'):
            nb, kc, n = W_SHAPES[name]
            wbn[0] ^= 1
            t = wb[wbn[0]]
            P.dma(eng, t[:, 0:kc * n], wbf[name][blk * 128:(blk + 1) * 128, :])
            return t[:, 0:kc * n].rearrange("p (k n) -> p k n", n=n)

        for t_, nm in ((Tm, 'Tm'), (U, 'U'), (onesblk, 'onesblk'), (cmneg, 'cmneg'), (cw, 'cw'), (cb, 'cb'),
                       (fcw, 'fcw'), (fcb, 'fcb'), (ng, 'ng')):
            P.dma('sync', t_[:], cd[nm] if nm in cd else sd[nm])
        P.dma('sync', hpb[:], sd['hp'].partition_broadcast(128)[:, 0])
        P.dma('sync', kvg[:], sd['kvg'].partition_broadcast(128)[:, 0])
        P.dma('sync', ikp[:], sd['ikp'].partition_broadcast(128)[:, 0])
        P.dma('sync', s16[:, 0:128], cd['identf'])
        k.copy(V, identb[:], s16[:, 0:128])
        k.act(Aneg[:], hpb[:, 1, :], AF.Exp)
        k.ts(V, Aneg[:], Aneg[:], -1.0, None, ALU.mult)
        P.dma('sync', s16[:, 0:2048], sd['wukT'].rearrange("p a b -> p (a b)"))
        k.copy(V, wukT[:].rearrange("p a b -> p (a b)"), s16[:, 0:2048])
        P.dma('sync', s16[:, 2048:4096], sd['wuv'].rearrange("p a b -> p (a b)"))
        k.copy(G, wuv[:].rearrange("p a b -> p (a b)"), s16[:, 2048:4096])
        k.memset(V, hal[:], 0.0)
        k.memset(V, fhal[:], 0.0)
        k.memset(V, S[:], 0.0)
        k.memset(G, kvtok[:, :, 256:258], 1.0)
        k.memset(V, pmask[:], 0.0)
        k.memset(V, pmask[0:64, 0:1], 1.0)
        k.memset(V, pmask[64:128, 1:2], 1.0)
        for it in range(NBIS + 1):
            k.memset(G, pw2[:, it:it + 1], 2.0 ** (-it))
        rb = fs[0][0:32, 0:16]
        P.dma('sync', rb, sd['relb'])
        oh = fs[1][0:32, 0:512].rearrange("p (a b) -> p a b", a=2)
        P.dma('sync', oh, cd['ohb'].rearrange("a p b -> p a b"))
        o32 = fs[2][0:32, 0:128]
        P.dma('sync', o32, cd['ones32'])
        ybf = ybuf[:].rearrange("p a b -> p (a b)").bitcast(F32)
        for kk in range(2):
            rhsb = ybf[0:32, 0:4096].rearrange("p (h j) -> p h j", h=16)
            for h in range(16):
                k.ts(V, rhsb[:, h, :], oh[:, kk, :], rb[:, h:h + 1], None, ALU.mult)
            zs = s16[:, 0:4096]
            for b in range(8):
                p_ = nps()
                k.mm(p_[:], o32, ybf[0:32, b * 512:(b + 1) * 512])
                k.copy(V, zs[:, b * 512:(b + 1) * 512], p_[:])
            P.dma('sync', zd[kk], zs)
            src = bass.AP(tensor=zd_t, offset=kk * 128 * 4096 + 127, ap=[[4095, 128], [256, 16], [1, 128]])
            stg = ybf[:, 4096:6144].rearrange("p (h t) -> p h t", h=16)
            P.dma('sync', stg, src, reads=['zscr'], writes=[ybuf])
            k.ts(V, biasT[:, :, kk * 128:(kk + 1) * 128], stg, 8.0, None, ALU.mult)
        ci = 0
        for name, (nb, kc, n) in W_SHAPES.items():
            for b in range(nb):
                stg = s16[:, 0:kc * n] if ci % 2 == 0 else ybf[:, 0:kc * n]
                P.dma('sync' if (ci % 2 == 0 or OPT.get('nosq')) else 'scalar', stg, wd[name][b * 128:(b + 1) * 128, :])
                wbn[0] ^= 1
                o = wb[wbn[0]][:, 0:kc * n]
                k.copy([V, G, A][ci % 3], o, stg)
                P.dma('gpsimd', wbf[name][b * 128:(b + 1) * 128, :], o)
                ci += 1

        def rstd_from(var_ap, out_ap, scale, eps):
            k.ts(V, out_ap, var_ap, scale, eps, ALU.mult, ALU.add)
            k.act(out_ap, out_ap, AF.Ln)
            k.act(out_ap, out_ap, AF.Exp, scale=-0.5)

        def layernorm(src, dst, dstb=None):
            k.bn_stats(st[:, 0:6], src[:, 0:512])
            k.bn_stats(st[:, 6:12], src[:, 512:1024])
            k.bn_aggr(st[:, 12:14], st[:, 0:12].rearrange("p (a b) -> p a b", a=2))
            rstd_from(st[:, 13:14], st[:, 14:15], 1.0, EPS)
            k.ts(V, src, src, st[:, 12:13], st[:, 14:15], ALU.subtract, ALU.mult)
            k.tt(V, src, src, lnp[:, 0, :], ALU.mult)
            if dst is not None:
                k.tt(V, dst, src, lnp[:, 1, :], ALU.add)
                if dstb is not None:
                    k.copy(A, dstb, dst)
            else:
                k.tt(V, dstb, src, lnp[:, 1, :], ALU.add)

        def to_hT(i):
            p_ = nps()
            pb = psb(p_)
            for kc in range(8):
                k.tr(pb[:, kc * 128:(kc + 1) * 128], hb[:, kc * 128:(kc + 1) * 128], identb[:])
            k.copy(A, hT[:, :, i * 128:(i + 1) * 128], pb[:, 0:1024].rearrange("p (k t) -> p k t", k=8))

        def load_ln(name):
            P.dma('sync', lnp[:].rearrange("p a b -> p (a b)"),
                  sd[name].rearrange("o a b -> o (a b)").partition_broadcast(128)[:, 0])

        cvn = [0]

        def conv(p_in, cidx, out_ap, halo, wts, bias, ntap, func):
            h = ntap - 1
            cvn[0] ^= 1
            cbuf = fs[0][:, :] if cvn[0] else fs[4]
            acc = fs[1][:, :] if cvn[0] else fs[5]
            k.copy(V, cbuf[:, 0:h], halo[:, cidx, :])
            k.copy(A, cbuf[:, h:h + 512], p_in)
            if OPT.get('noA'):
                k.ts(V, acc[:, 0:512], cbuf[:, h:h + 512], wts[:, cidx, h:h + 1], bias[:, cidx:cidx + 1], ALU.mult, ALU.add)
            else:
                k.act(acc[:, 0:512], p_in, AF.Identity, bias=bias[:, cidx:cidx + 1], scale=wts[:, cidx, h:h + 1])
            k.copy(G, halo[:, cidx, :], cbuf[:, 512:512 + h])
            for tp in range(0, h):
                last = (tp == h - 1) and func is None
                k.stt(out_ap if last else acc[:, 0:512], cbuf[:, tp:tp + 512], wts[:, cidx, tp:tp + 1],
                      acc[:, 0:512], ALU.mult, ALU.add)
            if func is not None:
                k.act(out_ap, acc[:, 0:512], func)

        def fm_proj(wv, m, p_):
            for kc in range(8):
                k.mm(p_[:], wv[:, kc, m * 128:(m + 1) * 128], hT[:, kc, :], start=(kc == 0), stop=(kc == 7))

        szg = sA[:].bitcast(F32).rearrange("p (i c) -> p i c", i=4)
        qT = sA[:].rearrange("p (k t) -> p k t", k=8)
        sgs = sA[:, 0:2048].rearrange("p (k t) -> p k t", k=4)
        sga = sA[:, 2048:4096].rearrange("p (k t) -> p k t", k=4)
        sBf = sB[:].bitcast(F32)
        rhsD = sBf[:, 0:1024]
        expD = sBf[:, 1024:2048]
        maskT = sB[:].rearrange("p (k t) -> p k t", t=128)
        Xg = sC[:, 0:2048].rearrange("p (i c) -> p i c", i=4)
        Bg = sC[:, 2048:2560].rearrange("p (i c) -> p i c", i=4)
        BT = sC[:, 2560:3072]
        CT = sC[:, 3072:3584]
        qlT = sC[:].rearrange("p (c h t) -> p c h t", c=2, h=16)
        mT = sC[:].rearrange("p (k t) -> p k t", k=8)
        Gm = sD[:, 0:1024]
        Xdt = sD[:, 1024:1536]
        Xd = sD[:, 1536:2048]
        S0b = sD[:, 2048:2560]
        S1b = sD[:, 2560:3072]
        vn = sD[:, 3072:3584]
        qiT = sD[:, 0:2048].rearrange("p (k t) -> p k t", k=4)
        PTs = [sD[:, 2048:2560], sD[:, 2560:3072]]
        mblk = sD[:, 3072:3584]
        ol = sD[:, 3584:3840]
        oT = sD[:, 3840:4096].rearrange("p (c t) -> p c t", c=2)
        ysT = ybuf[:, 0:16, :]
        yaT = ybuf[:, 16:24, :]
        gT = ybuf
        score = s16
        h1 = s16[:].rearrange("p (i c) -> p i c", i=4)

        def bc(ap, n):
            return ap.unsqueeze(2).to_broadcast([128, ap.shape[1], n])

        def chk(n):
            if limit <= n:
                raise _Stop()

        try:
          chk(1)
          for sti in range(NST):
              load_ln('ln0')
              for i in range(4):
                  Tg = sti * 4 + i
                  P.dma('sync', xt[:], x_d[Tg * 128:(Tg + 1) * 128, :])
                  layernorm(xt[:], None, hb[:])
                  to_hT(i)
              if sti == 0:
                  dump("hT", hT[:], [128, 8, 512], BF16)
              chk(2)
              wv = load_w('wsm', 0)
              for i in range(4):
                  Tg = sti * 4 + i
                  p_ = nps()
                  for kc in range(8):
                      k.mm(p_[:, 0:360], hT[:, kc, i * 128:(i + 1) * 128], wv[:, kc, :], start=(kc == 0), stop=(kc == 7))
                  k.tt(V, st[:, 16:48], p_[:, 0:32], hpb[:, 0, :], ALU.add)
                  k.act(st[:, 16:48], st[:, 16:48], AF.Exp)
                  k.act(dts[:, i, :], st[:, 16:48], AF.Ln, bias=1.0)
                  k.act(fs[2][:, 0:256], p_[:, 32:288], AF.Square, accum=st[:, 48:49])
                  rstd_from(st[:, 48:49], st[:, 49:50], 1.0 / 256, EPS)
                  k.ts(V, fs[2][:, 0:256], p_[:, 32:288], st[:, 49:50], None, ALU.mult)
                  k.tt(V, kvtok[:, Tg, 0:256], fs[2][:, 0:256], kvg[:], ALU.mult)
                  p2 = nps()
                  pb = psb(p2)
                  for cc in range(2):
                      k.tr(pb[:, cc * 128:(cc + 1) * 128], kvtok[:, Tg, cc * 128:(cc + 1) * 128], identb[:])
                  k.copy(A, kvnT[:, :, Tg * 128:(Tg + 1) * 128], pb[:, 0:256].rearrange("p (c t) -> p c t", c=2))
                  k.bn_stats(st[:, 50:56], p_[:, 288:352])
                  k.bn_aggr(st[:, 56:58], st[:, 50:56])
                  rstd_from(st[:, 57:58], st[:, 58:59], 1.0, EPS)
                  k.ts(V, fs[3][:, 0:64], p_[:, 288:352], st[:, 56:57], st[:, 58:59], ALU.subtract, ALU.mult)
                  k.tt(V, fs[3][:, 0:64], fs[3][:, 0:64], ikp[:, 0, :], ALU.mult)
                  k.tt(V, hb[:, 0:64], fs[3][:, 0:64], ikp[:, 1, :], ALU.add)
                  k.copy(V, hb[:, 64:128], hb[:, 0:64])
                  p3 = nps()
                  pb3 = psb(p3)
                  k.tr(pb3[:, 0:128], hb[:, 0:128], identb[:])
                  k.copy(A, kidxT[:, Tg * 128:(Tg + 1) * 128], pb3[:, 0:128])
                  k.ts(V, wi[:, i, :], p_[:, 352:360], (8 ** -0.5) * (64 ** -0.5), None, ALU.mult)
              if sti == 0:
                  dump("dts", dts[:], [128, 4, 32])
                  dump("kvtok", kvtok[:, 0:4, :], [128, 4, 258], BF16)
                  dump("kidxT", kidxT[:, 0:512], [128, 512], BF16)
              chk(3)
              sAb = sA[:]
              gsets = []
              for q_ in range(2):
                  base = sC[:] if q_ == 0 else ybuf[:, 16:24, :].rearrange("p a b -> p (a b)")
                  gsets.append(dict(
                      Xg=base[:, 0:2048].rearrange("p (i c) -> p i c", i=4),
                      Bg=base[:, 2048:2560].rearrange("p (i c) -> p i c", i=4),
                      BT=base[:, 2560:3072], CT=base[:, 3072:3584],
                      xsT=base[:, 3584:4096],
                      sz=sAb[:, q_ * 2048:(q_ + 1) * 2048].rearrange("p (i c) -> p i c", i=4)))

              pjb = [0]

              def pnps():
                  pjb[0] ^= 1
                  return ps[pjb[0]]

              def proj_gen(g):
                  gs = gsets[g % 2]
                  wv = load_w('wxs', g)
                  for j in range(4):
                      p_ = pnps()
                      fm_proj(wv, j, p_)
                      yield
                      conv(p_[:], 4 * g + j, gs['xsT'], hal, cw, cb, 4, AF.Silu)
                      yield
                      p2 = pnps()
                      pb = psb(p2)
                      for i in range(4):
                          k.tr(pb[:, i * 128:(i + 1) * 128], gs['xsT'][:, i * 128:(i + 1) * 128], identb[:])
                      k.copy(A, gs['Xg'][:, :, j * 128:(j + 1) * 128], pb[:, 0:512].rearrange("p (i c) -> p i c", i=4))
                      yield
                  wv = load_w('wbc', g)
                  p_ = pnps()
                  fm_proj(wv, 0, p_)
                  yield
                  conv(p_[:], 16 + g, gs['BT'], hal, cw, cb, 4, AF.Silu)
                  yield
                  p2 = pnps()
                  pb = psb(p2)
                  for i in range(4):
                      k.tr(pb[:, i * 128:(i + 1) * 128], gs['BT'][:, i * 128:(i + 1) * 128], identb[:])
                  k.copy(A, gs['Bg'], pb[:, 0:512].rearrange("p (i c) -> p i c", i=4))
                  yield
                  p_ = pnps()
                  fm_proj(wv, 1, p_)
                  yield
                  conv(p_[:], 20 + g, gs['CT'], hal, cw, cb, 4, AF.Silu)
                  yield
                  wv = load_w('wz', g)
                  for i in range(4):
                      p_ = pnps()
                      for kc in range(8):
                          k.mm(p_[:], hT[:, kc, i * 128:(i + 1) * 128], wv[:, kc, :], start=(kc == 0), stop=(kc == 7))
                      k.act(gs['sz'][:, i, :], p_[:], AF.Silu)
                      yield

              def iter_gen(g, i):
                  gs = gsets[g % 2]
                  Xg_, Bg_, BT_, CT_, szg_ = gs['Xg'], gs['Bg'], gs['BT'], gs['CT'], gs['sz']
                  g8 = slice(8 * g, 8 * g + 8)
                  tc_ = slice(i * 128, (i + 1) * 128)
                  par = i % 2
                  ba, bb_, bc_ = (ps[2], ps[3], ps[4]) if par == 0 else (ps[5], ps[6], ps[7])
                  sv = sst[par]
                  if par == 0:
                      rhsD_, expD_, Gm_, Xdt_, Xd0_, Xd1_ = rhsD, expD, Gm, Xdt, Xd, sD[:, 3584:4096]
                      S0b_, S1b_, vn_, t1, yv = S0b, S1b, vn, fs[2][:, 0:512], fs[3][:, 0:512]
                  else:
                      s16b = s16[:].bitcast(BF16)
                      jb = junk[:].bitcast(BF16)
                      rhsD_, expD_ = s16[:, 0:1024], s16[:, 1024:2048]
                      t1, yv = s16[:, 2048:2560], s16[:, 2560:3072]
                      Gm_, Xdt_, Xd0_ = s16b[:, 6144:7168], s16b[:, 7168:7680], s16b[:, 7680:8192]
                      Xd1_, S0b_, S1b_, vn_ = jb[:, 0:512], jb[:, 512:1024], jb[:, 1024:1536], jb[:, 1536:2048]
                  cbm = cbms[par][:]
                  a8 = sv[:, 0:8]
                  k.tt(V, a8, dts[:, i, g8], Aneg[:, g8], ALU.mult)
                  pE = ba
                  k.mm(pE[:, 0:8], Tm[:], a8)
                  k.mm(pE[:, 8:16], U[:], a8)
                  k.mm(pE[:, 16:24], onesblk[:, 0:128], a8)
                  k.mm(pE[:, 24:32], onesblk[:, 128:256], a8)
                  EX = sv[:, 8:40]
                  k.act(EX, pE[:, 0:32], AF.Exp)
                  yield
                  k.ts(V, sv[:, 40:48], EX[:, 8:16], pmask[:, 0:1], None, ALU.mult)
                  k.ts(V, sv[:, 48:56], EX[:, 8:16], pmask[:, 1:2], None, ALU.mult)
                  rD3 = rhsD_.rearrange("p (h l) -> p h l", h=8)
                  k.tt(V, rD3, bc(a8, 128), Tm[:].unsqueeze(1).to_broadcast([128, 8, 128]), ALU.mult)
                  yield
                  for b in range(2):
                      pD = bb_ if b == 0 else bc_
                      k.mm(pD[:], U[:], rhsD_[:, b * 512:(b + 1) * 512])
                      k.act(expD_[:, b * 512:(b + 1) * 512], pD[:], AF.Exp)
                  yield
                  pC = ba
                  k.mm(pC[:, 0:128], BT_[:, tc_], CT_[:, tc_])
                  k.tt(V, cbm, pC[:, 0:128], Tm[:], ALU.mult)
                  yield
                  k.tt(V, Gm_.rearrange("p (h l) -> p h l", h=8), expD_.rearrange("p (h l) -> p h l", h=8),
                       cbm.unsqueeze(1).to_broadcast([128, 8, 128]), ALU.mult)
                  X3 = Xg_[:, i, :].rearrange("p (h q) -> p h q", h=8)
                  Xdt3 = Xdt_.rearrange("p (h q) -> p h q", h=8)
                  k.tt(G, Xdt3, X3, bc(dts[:, i, g8], 64), ALU.mult)
                  yield
                  k.tt(G, Xd0_.rearrange("p (h q) -> p h q", h=8), Xdt3, bc(sv[:, 40:48], 64), ALU.mult)
                  k.tt(G, Xd1_.rearrange("p (h q) -> p h q", h=8), Xdt3, bc(sv[:, 48:56], 64), ALU.mult)
                  yield 'B'
                  pY = ba
                  for h in range(8):
                      k.mm(pY[:, h * 64:(h + 1) * 64], Gm_[:, h * 128:(h + 1) * 128], Xdt_[:, h * 64:(h + 1) * 64])
                  pS0 = bb_
                  pS1 = bc_
                  k.mm(pS0[:], Bg_[:, i, :], Xd0_)
                  k.mm(pS1[:], Bg_[:, i, :], Xd1_)
                  yield
                  Sg = S[:, g, :]
                  Sg3 = Sg.rearrange("p (h q) -> p h q", h=8)
                  k.copy(A, S0b_, Sg)
                  k.tt(V if True else G, Sg3, Sg3, bc(EX[:, 16:24], 64), ALU.mult)
                  k.tt(V, Sg, Sg, pS0[:], ALU.add)
                  yield
                  k.copy(A, S1b_, Sg)
                  k.tt(V if True else G, Sg3, Sg3, bc(EX[:, 24:32], 64), ALU.mult)
                  k.tt(V, Sg, Sg, pS1[:], ALU.add)
                  yield
                  pO = bb_
                  k.mm(pO[0:64, :], CT_[:, i * 128:i * 128 + 64], S0b_)
                  k.mm(pO[64:128, :], CT_[:, i * 128 + 64:(i + 1) * 128], S1b_)
                  k.tt(V, t1.rearrange("p (h q) -> p h q", h=8), pO[:].rearrange("p (h q) -> p h q", h=8),
                       bc(EX[:, 0:8], 64), ALU.mult)
                  yield
                  k.tt(V, yv, t1, pY[:], ALU.add)
                  k.tt(G, t1.rearrange("p (h q) -> p h q", h=8), X3, bc(hpb[:, 2, g8], 64), ALU.mult)
                  yield
                  k.tt(V if True else G, yv, yv, t1, ALU.add)
                  k.tt(V if True else G, yv, yv, szg_[:, i, :], ALU.mult)
                  k.act(t1, yv, AF.Square, accum=sv[:, 56:57])
                  yield
                  rstd_from(sv[:, 56:57], sv[:, 57:58], 1.0 / 512, EPS)
                  k.ts(V, vn_, yv, sv[:, 57:58], None, ALU.mult)
                  yield
                  p2 = bc_
                  pb = psb(p2)
                  for j in range(4):
                      k.tr(pb[:, j * 128:(j + 1) * 128], vn_[:, j * 128:(j + 1) * 128], identb[:])
                  for j in range(4):
                      k.ts(V, ysT[:, 4 * g + j, tc_], pb[:, j * 128:(j + 1) * 128],
                           ng[:, 4 * g + j:4 * g + j + 1], None, ALU.mult)
                  yield

              def run_all(gen):
                  for _ in gen:
                      pass

              def run_until_B(gen):
                  for r in gen:
                      if r == 'B':
                          return

              def interleave(gens):
                  gens = [g_ for g_ in gens if g_ is not None]
                  while gens:
                      alive = []
                      for g_ in gens:
                          try:
                              r = next(g_)
                              alive.append(g_)
                          except StopIteration:
                              pass
                      gens = alive

              def take(gen, n):
                  def sub():
                      for _ in range(n):
                          try:
                              next(gen)
                          except StopIteration:
                              return
                          yield
                  return sub()

              def untilB(gen):
                  def sub():
                      for r in gen:
                          if r == 'B':
                              return
                          yield
                  return sub()

              run_all(proj_gen(0))
              for g in range(4):
                  its = [iter_gen(g, i) for i in range(4)]
                  pj = proj_gen(g + 1) if g < 3 else None
                  run_until_B(its[0])
                  for n in range(4):
                      nxt = untilB(its[n + 1]) if n < 3 else None
                      pjs = take(pj, 6) if pj is not None else None
                      interleave([its[n], nxt, pjs])
                  if pj is not None:
                      run_all(pj)
                  if sti == 0 and g == 0:
                      dump("Xg", gsets[0]['Xg'], [128, 4, 512], BF16)
                      dump("CT", gsets[0]['CT'], [128, 512], BF16)
              if sti == 0:
                  dump("ysT", ysT, [128, 16, 512], BF16)
              chk(4)
              for b in range(2):
                  wv = load_w('wq', b)
                  for m in range(4):
                      p_ = nps()
                      fm_proj(wv, m, p_)
                      k.copy(A, qT[:, b * 4 + m, :], p_[:])
              wv = load_w('wqi', 0)
              for m in range(4):
                  p_ = nps()
                  fm_proj(wv, m, p_)
                  k.copy(A, qiT[:, m, :], p_[:])
              chk(4.1)
              psmod[0] = 4
              k.memset(V, qTm[64:128, 0, :, :], 0.0)
              k.memset(G, qTm[0:64, 1, :, :], 0.0)
              k.memset(V, qiTm[64:128, 0, :, :], 0.0)
              k.memset(G, qiTm[0:64, 1, :, :], 0.0)
              maskTs = [maskT, maskT2[:].rearrange("p (k t) -> p k t", t=128)]

              def idx_a(i):
                  Tg = sti * 4 + i
                  nk = (Tg + 1) * 128
                  tc_ = slice(i * 128, (i + 1) * 128)
                  nb4 = (nk + 511) // 512
                  k.copy(V, qiTm[0:64, 0, :, :], qiT[0:64, :, tc_])
                  k.copy(G, qiTm[64:128, 1, :, :], qiT[64:128, :, tc_])
                  for kb4 in range(nb4):
                      c0 = kb4 * 512
                      ncol = min(512, nk - c0)
                      for h in range(8):
                          p_ = nps()
                          k.mm(p_[:, 0:ncol], qiTm[:, h % 2, h // 2, :], kidxT[:, c0:c0 + ncol])
                          rl = fs[h % 2][:, 0:ncol]
                          k.act(rl, p_[:, 0:ncol], AF.Relu)
                          if h == 0:
                              k.ts(V, score[:, c0:c0 + ncol], rl, wi[:, i, 0:1], None, ALU.mult)
                          else:
                              k.stt(score[:, c0:c0 + ncol], rl, wi[:, i, h:h + 1], score[:, c0:c0 + ncol],
                                    ALU.mult, ALU.add)
                  thr = bst[:, 2:3]
                  if nk > 256:
                      Bm = bst[:, 4:5]
                      P.add(V, lambda e: e.tensor_reduce(Bm, score[:, 0:nk], AX.X, ALU.max, apply_absolute_value=True),
                            reads=[score[:, 0:nk]], writes=[Bm])
                      k.ts(V, Bm, Bm, 1e-20, 1.0000001, ALU.add, ALU.mult)
                      hwt = bst[:, 8:8 + NBIS + 1]
                      k.ts(V, hwt, pw2[:, 0:NBIS + 1], Bm, None, ALU.mult)
                      hwn = bst[:, 32:32 + NBIS + 1]
                      k.ts(V, hwn, hwt, -0.5, None, ALU.mult)
                  k.tt(V, score[:, Tg * 128:(Tg + 1) * 128], score[:, Tg * 128:(Tg + 1) * 128], cmneg[:], ALU.add)
                  if nk > 256:
                      k.memset(V, bst[:, 0:1], 0.0)
                  else:
                      k.memset(V, thr, -10000.0)

              def bis_part(i, it0, it1):
                  Tg = sti * 4 + i
                  nk = (Tg + 1) * 128
                  if nk <= 256:
                      return
                  mid = bst[:, 0:1]
                  cnt = bst[:, 1:2]
                  dd = bst[:, 3:4]
                  hwt = bst[:, 8:8 + NBIS + 1]
                  hwn = bst[:, 32:32 + NBIS + 1]
                  for it in range(it0, it1):
                      k.ts(V, junk[:, 0:nk], score[:, 0:nk], mid, None, ALU.is_ge, ALU.add, accum=cnt)
                      k.ts(V, dd, cnt, 255.5, hwt[:, it:it + 1], ALU.is_ge, ALU.mult)
                      k.stt(mid, dd, hwn[:, it:it + 1], mid, ALU.add, ALU.add)
                  if it1 == NBIS:
                      k.tt(V, bst[:, 2:3], mid, hwt[:, NBIS:NBIS + 1], ALU.subtract)

              def idx_b(i):
                  Tg = sti * 4 + i
                  nk = (Tg + 1) * 128
                  nb4 = (nk + 511) // 512
                  mT_ = maskTs[i % 2]
                  thr = bst[:, 2:3]
                  for kb4 in range(nb4):
                      c0 = kb4 * 512
                      ncol = min(512, nk - c0)
                      mb_ = mblk if kb4 % 2 == 0 else sD[:, 3584:4096]
                      k.ts(V, mb_[:, 0:ncol], score[:, c0:c0 + ncol], thr, None, ALU.is_ge)
                      p2 = nps()
                      pb = psb(p2)
                      for q in range(ncol // 128):
                          k.tr(pb[:, q * 128:(q + 1) * 128], mb_[:, q * 128:(q + 1) * 128], identb[:])
                      k.copy(A, mT_[:, kb4 * 4:kb4 * 4 + ncol // 128, :],
                             pb[:, 0:ncol].rearrange("p (q t) -> p q t", t=128))

              bias8 = biasT
              PT3s = [sD[:, 2048:2560], sD[:, 2560:3072], sD[:, 3072:3584]]
              Lb = [ps[1], ps[2], ps[3]]
              olb = hb[:, 0:256]
              oTb = hb[:, 256:512].rearrange("p (c t) -> p c t", c=2)

              def main_stage(i):
                  Tg = sti * 4 + i
                  tc_ = slice(i * 128, (i + 1) * 128)
                  mT_ = maskTs[i % 2]
                  k.copy(V, qTm[0:64, 0, :, :], qT[0:64, :, tc_])
                  k.copy(G, qTm[64:128, 1, :, :], qT[64:128, :, tc_])
                  for hg in range(4):
                      for cc in range(2):
                          p_ = ps[0]
                          for hh in range(4):
                              h = 4 * hg + hh
                              k.mm(p_[:, hh * 128:(hh + 1) * 128], wukT[:, h // 2, cc * 128:(cc + 1) * 128],
                                   qTm[:, h % 2, h // 2, :])
                          k.copy(A, qlT[:, cc, 4 * hg:4 * hg + 4, :], p_[:].rearrange("p (h t) -> p h t", h=4))
                  tot = 4 * (Tg + 1)
                  done = [0, 0]

                  def tick():
                      done[0] += 1
                      if i < 3:
                          want = min(NBIS, (done[0] * NBIS + tot - 1) // tot)
                          if want > done[1]:
                              bis_part(i + 1, done[1], want)
                              done[1] = want

                  for hgx in range(4):
                      its = [(hgx, kb) for kb in range(Tg + 1)]
                      run_its(its, Tg, tc_, mT_, tick)

              def run_its(its, Tg, tc_, mT_, tick):
                  def qk(n):
                      hg, kb = its[n]
                      pL = Lb[n % 3]
                      near = kb >= Tg - 1
                      for cc in range(2):
                          k.mm(pL[:], kvnT[:, cc, kb * 128:(kb + 1) * 128], qlT[:, cc, 4 * hg:4 * hg + 4, :],
                               start=(cc == 0), stop=(cc == 1 and not near))
                      if near:
                          off = 128 if kb == Tg else 0
                          k.mm(pL[:], identb[:], bias8[:, 4 * hg:4 * hg + 4, off:off + 128], start=False, stop=True)

                  def softmax_part(n):
                      hg, kb = its[n]
                      pL = Lb[n % 3]
                      PT = PT3s[n % 3]
                      k.act(PT, pL[:], AF.Exp, scale=0.125)
                      PT3 = PT.rearrange("p (h t) -> p h t", h=4)
                      k.tt(G if (n % 2 == 0 or True) else V, PT3, PT3, mT_[:, kb, :].unsqueeze(1).to_broadcast([128, 4, 128]), ALU.mult)

                  def pv(n):
                      hg, kb = its[n]
                      PT = PT3s[n % 3]
                      for hh in range(4):
                          k.mm(ps[4 + hh][:, 0:257], PT[:, hh * 128:(hh + 1) * 128], kvtok[:, kb, 0:257],
                               start=(kb == 0), stop=(kb == Tg))
                      if kb == Tg:
                          for hh in range(4):
                              h = 4 * hg + hh
                              po = 64 * (h % 2)
                              acc = ps[4 + hh]
                              rv = bst[:, 60 + hh:61 + hh]
                              k.recip(rv, acc[:, 256:257])
                              k.ts(V, olb, acc[:, 0:256], rv, None, ALU.mult)
                              pb = psb(ps[0])
                              for cc in range(2):
                                  k.tr(pb[:, cc * 128:(cc + 1) * 128], olb[:, cc * 128:(cc + 1) * 128], identb[:])
                              k.copy(A, oTb, pb[:, 0:256].rearrange("p (c t) -> p c t", c=2))
                              pYa = ps[0]
                              for cc in range(2):
                                  k.mm(pYa[po:po + 64, 256:384], wuv[:, cc, h * 64:(h + 1) * 64], oTb[:, cc, :],
                                       start=(cc == 0), stop=(cc == 1))
                              if h % 2 == 1:
                                  k.copy(A, yaT[:, h // 2, tc_], pYa[:, 256:384])

                  N = len(its)
                  LA = 2
                  for n in range(min(LA, N)):
                      qk(n)
                  for n in range(N):
                      softmax_part(n)
                      if n + LA < N:
                          qk(n + LA)
                      pv(n)
                      tick()

              idx_a(0)
              bis_part(0, 0, NBIS)
              idx_b(0)
              for i in range(4):
                  if i < 3:
                      idx_a(i + 1)
                  main_stage(i)
                  if i < 3:
                      idx_b(i + 1)
              psmod[0] = 8
              if sti == 0:
                  dump("yaT", yaT, [128, 8, 512], BF16)
              chk(5)
              for q4 in range(2):
                  wv = load_w('wgs', q4)
                  for m in range(4):
                      p_ = nps()
                      fm_proj(wv, m, p_)
                      k.act(sgs[:, m, :], p_[:], AF.Sigmoid)
                  wv = load_w('wga', q4)
                  for m in range(4):
                      p_ = nps()
                      fm_proj(wv, m, p_)
                      k.act(sga[:, m, :], p_[:], AF.Sigmoid)
                  for b2 in range(2):
                      wv = load_w('wbs', 2 * q4 + b2)
                      for m in range(2):
                          p_ = nps()
                          for kc in range(16):
                              k.mm(p_[:], wv[:, kc, m * 128:(m + 1) * 128], ysT[:, kc, :], start=(kc == 0), stop=(kc == 15))
                          k.tt(V, mT[:, 4 * q4 + 2 * b2 + m, :], p_[:], sgs[:, 2 * b2 + m, :], ALU.mult)
                  wv = load_w('wba', q4)
                  for m in range(4):
                      p_ = nps()
                      for kc in range(8):
                          k.mm(p_[:], wv[:, kc, m * 128:(m + 1) * 128], yaT[:, kc, :], start=(kc == 0), stop=(kc == 7))
                      k.tt(V, fs[2][:, 0:512], p_[:], sga[:, m, :], ALU.mult)
                      k.tt(V, mT[:, 4 * q4 + m, :], mT[:, 4 * q4 + m, :], fs[2][:, 0:512], ALU.add)
              if sti == 0:
                  dump("mT", mT, [128, 8, 512], BF16)
              wo = [load_w('wo', 0), load_w('wo', 1)]
              load_ln('ln0')
              for i in range(4):
                  Tg = sti * 4 + i
                  P.dma('sync', xt[:], x_d[Tg * 128:(Tg + 1) * 128, :])
                  layernorm(xt[:], xt[:])
                  for b2 in range(2):
                      p_ = nps()
                      for kc in range(8):
                          k.mm(p_[:], mT[:, kc, i * 128:(i + 1) * 128], wo[b2][:, kc, :], start=(kc == 0), stop=(kc == 7))
                      k.stt(h1[:, i, b2 * 512:(b2 + 1) * 512], xt[:, b2 * 512:(b2 + 1) * 512], ALPHA, p_[:],
                            ALU.mult, ALU.add)
              load_ln('ln1')
              for i in range(4):
                  layernorm(h1[:, i, :], h1[:, i, :], hb[:])
                  to_hT(i)
              if sti == 0:
                  dump("h1", h1, [128, 4, 1024])
              chk(6)
              for blk in range(11):
                  wv = load_w('wup', blk)
                  for jj in range(2):
                      j = 2 * blk + jj
                      p_ = nps()
                      fm_proj(wv, 2 * jj, p_)
                      ga = fs[2][:, 0:512]
                      conv(p_[:], j, ga, fhal, fcw, fcb, 3, AF.Silu)
                      p2 = nps()
                      fm_proj(wv, 2 * jj + 1, p2)
                      gv = fs[3][:, 0:512]
                      conv(p2[:], 22 + j, gv, fhal, fcw, fcb, 3, None)
                      k.tt(V, gT[:, j, :], ga, gv, ALU.mult)
              for nb in range(8):
                  wv = load_w('wdn', nb)
                  p_ = nps()
                  for i in range(4):
                      for j in range(22):
                          k.mm(p_[:, i * 128:(i + 1) * 128], gT[:, j, i * 128:(i + 1) * 128], wv[:, j, :],
                               start=(j == 0), stop=(j == 21))
                  for i in range(4):
                      k.stt(h1[:, i, nb * 128:(nb + 1) * 128], h1[:, i, nb * 128:(nb + 1) * 128], ALPHA,
                            p_[:, i * 128:(i + 1) * 128], ALU.mult, ALU.add)
              load_ln('ln2')
              for i in range(4):
                  Tg = sti * 4 + i
                  layernorm(h1[:, i, :], h1[:, i, :])
                  P.dma('sync', out_d[Tg * 128:(Tg + 1) * 128, :], h1[:, i, :])
        except _Stop:
            pass
        P.emit()
    return nc


def make_inmaps(inputs, L, nb):
    cw_ = host_consts()
    ww = host_weights(inputs)
    ss = host_small(inputs)
    base = {}
    for kk, v in ww.items():
        base[kk] = np.ascontiguousarray(v.reshape(v.shape[0] * 128, -1))
    base.update(ss)
    base.update(cw_)
    x = np.asarray(inputs['x'], np.float32)
    maps = []
    for c in range(nb):
        m = dict(base)
        m['x'] = np.ascontiguousarray(x[c % x.shape[0], :L])
        maps.append(m)
    return maps


def kernel(**inputs):
    L = SEQ
    nc = build(L)
    maps = make_inmaps(inputs, L, 8)
    res = run_bass_kernel_spmd(nc, maps, core_ids=list(range(8)))
    out = np.stack([np.asarray(res.results[c]["out"], np.float32) for c in range(4)], axis=0)
    return out
```

```python
import math
from contextlib import ExitStack
import numpy as np
import concourse.bass as bass
import concourse.mybir as mybir
from concourse.bass_utils import run_bass_kernel_spmd

F32 = mybir.dt.float32
BF16 = mybir.dt.bfloat16
U8 = mybir.dt.uint8
ALU = mybir.AluOpType
AF = mybir.ActivationFunctionType
AX = mybir.AxisListType

ENGS = ['sync', 'scalar', 'vector', 'gpsimd', 'tensor']
NDS = 8


def _acc(x):
    if isinstance(x, (str, tuple)):
        return (x, 0, 1 << 30, 0, 1 << 60)
    name = x.name
    if not hasattr(x, 'offset') or not hasattr(x, 'ap'):
        return (name, 0, 1 << 30, 0, 1 << 60)
    sz = mybir.dt.size(x.dtype)
    ap = x.ap
    off = x.offset
    if name.startswith('ps'):
        return (name, 0, 1 << 30, 0, 1 << 60)
    if name.startswith('sb_'):
        pitch = ap[0][0]
        if pitch <= 0:
            return (name, 0, 1 << 30, 0, 1 << 60)
        plo = off // pitch
        phi = plo + ap[0][1]
        lo = off % pitch
        hi = lo + sum((c - 1) * abs(st_) for st_, c in ap[1:]) + 1
        return (name, plo, phi, lo * sz, hi * sz)
    hi = off + sum((c - 1) * abs(st_) for st_, c in ap) + 1
    return (name, 0, 1, off * sz, hi * sz)


def _ovl(a, b):
    return a[1] < b[2] and b[1] < a[2] and a[3] < b[4] and b[3] < a[4]


def _cov(a, b):
    return a[1] <= b[1] and b[2] <= a[2] and a[3] <= b[3] and b[4] <= a[4]


class Prog:
    def __init__(self, nc, es):
        self.nc = nc
        self.es = es
        self.ol = []
        self.W = {}
        self.R = {}
        self.sems = {}

    def sem(self, k):
        if k not in self.sems:
            self.sems[k] = self.es.enter_context(self.nc.semaphore("s_" + "_".join(str(a) for a in k)))
        return self.sems[k]

    def add(self, eng, fn, reads=(), writes=(), dma=False, cost=0.3):
        oid = len(self.ol)
        deps = {}
        ra = [_acc(k) for k in reads]
        wa = [_acc(k) for k in writes]
        for a in ra:
            for w in self.W.get(a[0], ()):
                if _ovl(w, a):
                    deps[w[5]] = True
        for a in wa:
            for lst in (self.W.get(a[0], ()), self.R.get(a[0], ())):
                for w in lst:
                    if _ovl(w, a) and w[5] != oid:
                        d = self.ol[w[5]]
                        pe_pe = (not dma) and (not d[2]) and d[0] == eng == 'tensor'
                        if w[5] not in deps:
                            deps[w[5]] = not pe_pe
        pin = OPT.get('pin', ('scalar',))
        if (eng in pin and not dma) or (dma and ('dma_' + eng) in pin):
            pk = ('dma_' + eng) if dma else eng
            lp = getattr(self, '_last', {}).get(pk)
            if lp is not None:
                deps.setdefault(lp, False)
            if not hasattr(self, '_last'):
                self._last = {}
            self._last[pk] = oid
        self.ol.append([eng, fn, dma, cost, deps])
        for a in wa:
            e = a + (oid,)
            self.W[a[0]] = [w for w in self.W.get(a[0], ()) if not _cov(a, w)] + [e]
            self.R[a[0]] = [r for r in self.R.get(a[0], ()) if not _cov(a, r)]
        for a in ra:
            e = a + (oid,)
            lst = self.R.get(a[0], [])
            if not dma:
                keep = []
                for r in lst:
                    if self.ol[r[5]][0] == eng and not self.ol[r[5]][2] and _cov(a, r) and r[5] != oid:
                        self.ol[oid][4].setdefault(r[5], False)
                    else:
                        keep.append(r)
                lst = keep
            lst.append(e)
            self.R[a[0]] = lst
        return oid

    def dma(self, eng, out, in_, reads=None, writes=None, **kw):
        nb = 0
        try:
            nb = out.nbytes() if callable(getattr(out, 'nbytes', None)) else out.nbytes
        except Exception:
            nb = 1 << 16
        return self.add(eng, lambda e: e.dma_start(out=out, in_=in_, **kw),
                        reads=[in_] if reads is None else reads,
                        writes=[out] if writes is None else writes, dma=True, cost=2.0 + nb / 150e3)

    def schedule(self):
        ol = self.ol
        n = len(ol)
        if OPT.get('nosched'):
            order = {e: [] for e in ENGS}
            for i, o in enumerate(ol):
                order[o[0]].append(i)
            return order
        succ = [[] for _ in range(n)]
        ndep = [0] * n
        for i, o in enumerate(ol):
            ndep[i] = len(o[4])
            for d in o[4]:
                succ[d].append(i)
        fin = [0.0] * n
        start_ok = [0.0] * n
        eng_free = {e: 0.0 for e in ENGS}
        ready = {e: [] for e in ENGS}
        import bisect
        for i in range(n):
            if ndep[i] == 0:
                ready[ol[i][0]].append(i)
        order = {e: [] for e in ENGS}
        done = 0
        lo = 0
        sched = [False] * n
        WIN = OPT.get('win', 700)
        NC = OPT.get('nc', 6)
        while done < n:
            best = None
            while lo < n and sched[lo]:
                lo += 1
            for e in ENGS:
                r = ready[e]
                c = 0
                for i in r:
                    if i >= lo + WIN:
                        break
                    est = max(eng_free[e], start_ok[i])
                    key = (est, i)
                    if best is None or key < best[0]:
                        best = (key, e, i)
                    c += 1
                    if c >= NC:
                        break
            if best is None:
                cand = [(ready[e][0], e) for e in ENGS if ready[e]]
                i, e = min(cand)
                best = ((max(eng_free[e], start_ok[i]), i), e, i)
            (est, _), e, i = best
            o = ol[i]
            ready[e].remove(i)
            sched[i] = True
            done += 1
            order[e].append(i)
            if o[2]:
                eng_free[e] = est + 0.15
                fin[i] = est + o[3]
            else:
                eng_free[e] = est + o[3]
                fin[i] = est + o[3]
            for sidx in succ[i]:
                lat = 0.05 if (ol[sidx][0] == e and not o[2]) else 0.3
                t = fin[i] + lat
                if t > start_ok[sidx]:
                    start_ok[sidx] = t
                ndep[sidx] -= 1
                if ndep[sidx] == 0:
                    bisect.insort(ready[ol[sidx][0]], sidx)
        return order

    def emit(self):
        nc = self.nc
        ol = self.ol
        order = self.schedule()
        tok = [None] * len(ol)
        cnt = {e: 0 for e in ENGS}
        dman = {e: 0 for e in ENGS}
        dmal = {e: [] for e in ENGS}
        plan = {e: [] for e in ENGS}
        for e in ENGS:
            for i in order[e]:
                o = ol[i]
                if o[2]:
                    k_ = dman[e]
                    dman[e] += 1
                    tok[i] = (('d', e, k_ % NDS), 16 * (k_ // NDS + 1), e, True)
                    if k_ >= NDS:
                        o[4][dmal[e][k_ - NDS]] = True
                    dmal[e].append(i)
                else:
                    cnt[e] += 1
                    if cnt[e] % 60000 == 0:
                        cnt[e] += 1
                    tok[i] = (('c', e, cnt[e] // 60000), cnt[e] % 60000, e, False)
        fin = {}
        for e in ENGS:
            waited = {}
            for i in order[e]:
                o = ol[i]
                waits = {}
                for d, need in o[4].items():
                    if not need:
                        continue
                    s_, v, _, _ = tok[d]
                    if waited.get(s_, 0) >= v:
                        continue
                    waits[s_] = max(waits.get(s_, 0), v)
                for s_, v in waits.items():
                    waited[s_] = v
                plan[e].append((list(waits.items()), o[1], tok[i]))
                fin[tok[i][0]] = max(fin.get(tok[i][0], 0), tok[i][1])
        for e in ENGS:
            for (ws, _, t) in plan[e]:
                self.sem(t[0])
                for a, _ in ws:
                    self.sem(a)
        with nc.Block() as block:
            def run(engname):
                def body(e):
                    for (waits, fn, t) in plan[engname]:
                        for s_, v in waits:
                            e.wait_ge(self.sem(s_), v)
                        ins = fn(e)
                        ins.then_inc(self.sem(t[0]), 16 if t[3] else 1)
                    if engname == 'sync':
                        for s_, v in fin.items():
                            e.wait_ge(self.sem(s_), v)
                return body
            block.sync(run('sync'))
            block.scalar(run('scalar'))
            block.vector(run('vector'))
            block.gpsimd(run('gpsimd'))
            block.tensor(run('tensor'))


OPT = {'mm': 2}
D_MODEL = 1024
SEQ = 4096
N_IN = 9064
D_FF = 2816
ALPHA = 2.0 ** 0.25
EPS = 1e-5
NEG = -30000.0
NBIS = 20
CHKT = 2
BIS_HW = 256.0

O_Z, O_XBC, O_DT, O_Q, O_CKV, O_QI, O_KI, O_WI, O_GS, O_GA = 0, 2048, 5120, 5152, 6176, 6432, 6944, 7008, 7016, 8040


def _t5_bucket(rel):
    nb = 16
    max_exact = 8
    bucket = np.where(rel > 0, nb, 0)
    n = np.abs(rel)
    nf = np.maximum(n, 1).astype(np.float32)
    large = max_exact + (np.log(nf / max_exact) / math.log(128 / max_exact) * (nb - max_exact)).astype(np.int32)
    large = np.minimum(large, nb - 1)
    return bucket + np.where(n < max_exact, n, large)


def _blk(W, cols, N):
    K = W.shape[0]
    cols = np.asarray(cols)
    nb = len(cols) // N
    Wc = W[:, cols].reshape(K // 128, 128, nb, N)
    return np.ascontiguousarray(Wc.transpose(2, 1, 0, 3)).astype(np.float32)


def host_consts():
    c = {}
    p = np.arange(128)
    same = (p[:, None] // 64) == (p[None, :] // 64)
    c['Tm'] = (same & (p[:, None] <= p[None, :])).astype(np.float32)
    c['U'] = (same & (p[:, None] > p[None, :])).astype(np.float32)
    ob = np.zeros((128, 256), np.float32)
    ob[:64, :128] = 1.0
    ob[64:, 128:] = 1.0
    c['onesblk'] = ob
    c['cmneg'] = np.where((p[None, :] // 64) <= (p[:, None] // 64), 0.0, NEG).astype(np.float32)
    c['identf'] = np.eye(128, dtype=np.float32)
    j = np.arange(256)
    oh = np.zeros((2, 32, 256), np.float32)
    for k, rel in enumerate([-1 - j, 127 - j]):
        b = _t5_bucket(rel)
        oh[k, b, j] = 1.0
        oh[k, 15, :] -= 1.0
    c['ohb'] = oh
    c['ones32'] = np.ones((32, 128), np.float32)
    return c


def host_weights(inp):
    w = {}
    w_in = np.asarray(inp['w_in'][0], np.float32)
    w['wsm'] = _blk(w_in, list(range(O_DT, O_DT + 32)) + list(range(O_CKV, O_CKV + 256)) +
                    list(range(O_KI, O_KI + 64)) + list(range(O_WI, O_WI + 8)), 360)
    w['wz'] = _blk(w_in, range(O_Z, O_Z + 2048), 512)
    w['wxs'] = _blk(w_in, range(O_XBC, O_XBC + 2048), 512)
    bc = []
    for g in range(4):
        bc += list(range(O_XBC + 2048 + g * 128, O_XBC + 2048 + (g + 1) * 128))
        bc += list(range(O_XBC + 2560 + g * 128, O_XBC + 2560 + (g + 1) * 128))
    w['wbc'] = _blk(w_in, bc, 256)
    w['wq'] = _blk(w_in, range(O_Q, O_Q + 1024), 512)
    w['wqi'] = _blk(w_in, range(O_QI, O_QI + 512), 512)
    w['wgs'] = _blk(w_in, range(O_GS, O_GS + 1024), 512)
    w['wga'] = _blk(w_in, range(O_GA, O_GA + 1024), 512)
    w['wbs'] = _blk(np.asarray(inp['w_br_ssd'][0], np.float32), range(1024), 256)
    w['wba'] = _blk(np.asarray(inp['w_br_att'][0], np.float32), range(1024), 512)
    w['wo'] = _blk(np.asarray(inp['w_out'][0], np.float32), range(1024), 512)
    up = []
    for j in range(22):
        up += list(range(j * 128, (j + 1) * 128)) + list(range(D_FF + j * 128, D_FF + (j + 1) * 128))
    w['wup'] = _blk(np.asarray(inp['ffn_w_up'][0], np.float32), up, 512)
    w['wdn'] = _blk(np.asarray(inp['ffn_w_down'][0], np.float32), range(1024), 128)
    return w


W_SHAPES = {'wsm': (1, 8, 360), 'wz': (4, 8, 512), 'wxs': (4, 8, 512), 'wbc': (4, 8, 256), 'wq': (2, 8, 512),
            'wqi': (1, 8, 512), 'wgs': (2, 8, 512), 'wga': (2, 8, 512), 'wbs': (4, 16, 256), 'wba': (2, 8, 512),
            'wo': (2, 8, 512), 'wup': (11, 8, 512), 'wdn': (8, 22, 128)}


def host_small(inp):
    s = {}
    f = lambda a: np.ascontiguousarray(np.asarray(a, np.float32))
    s['ln0'] = f(np.stack([inp['ln_in_g'], inp['ln_in_b']])[None])
    s['ln1'] = f(np.stack([inp['ln1_g'][0], inp['ln1_b'][0]])[None])
    s['ln2'] = f(np.stack([inp['ln2_g'][0], inp['ln2_b'][0]])[None])
    cw = np.asarray(inp['ssd_conv_w'][0], np.float32)
    s['cw'] = f(cw.T.reshape(24, 128, 4).transpose(1, 0, 2))
    s['cb'] = f(np.asarray(inp['ssd_conv_b'][0]).reshape(24, 128).T)
    fw = np.asarray(inp['ffn_conv_w'][0], np.float32)
    s['fcw'] = f(fw.T.reshape(44, 128, 3).transpose(1, 0, 2))
    s['fcb'] = f(np.asarray(inp['ffn_conv_b'][0]).reshape(44, 128).T)
    s['hp'] = f(np.stack([inp['ssd_dt_bias'][0], inp['ssd_A_log'][0], inp['ssd_D'][0]])[None])
    s['ng'] = f(np.asarray(inp['ssd_norm_g'][0]).reshape(16, 128).T)
    s['kvg'] = f(np.asarray(inp['att_kv_norm_g'][0])[None])
    s['ikp'] = f(np.stack([inp['idx_k_norm_g'][0], inp['idx_k_norm_b'][0]])[None])
    uk = np.asarray(inp['att_w_uk'][0], np.float32)
    s['wukT'] = f(uk.transpose(0, 2, 1).reshape(8, 128, 256).transpose(1, 0, 2))
    uv = np.asarray(inp['att_w_uv'][0], np.float32)
    s['wuv'] = f(uv.transpose(1, 0, 2).reshape(2, 128, 1024).transpose(1, 0, 2))
    s['relb'] = f(inp['rel_bias'])
    return s


S_SHAPES = {'ln0': (1, 2, 1024), 'ln1': (1, 2, 1024), 'ln2': (1, 2, 1024), 'cw': (128, 24, 4), 'cb': (128, 24),
            'fcw': (128, 44, 3), 'fcb': (128, 44), 'hp': (1, 3, 32), 'ng': (128, 16), 'kvg': (1, 256),
            'ikp': (1, 2, 64), 'wukT': (128, 8, 256), 'wuv': (128, 2, 1024), 'relb': (32, 16)}
C_SHAPES = {'Tm': (128, 128), 'U': (128, 128), 'onesblk': (128, 256), 'cmneg': (128, 128), 'identf': (128, 128),
            'ohb': (2, 32, 256), 'ones32': (32, 128)}


def _fs(ap):
    try:
        v = ap.free_size
        return v() if callable(v) else v
    except Exception:
        return 512


class K:
    def __init__(self, P):
        self.P = P

    @staticmethod
    def _aps(xs):
        return [x for x in xs if hasattr(x, 'name') and not isinstance(x, (str, float, int))]

    def mm(self, out, lhsT, rhs, start=True, stop=True):
        c_ = 0.06 + _fs(rhs) * 0.00045 * (4 if rhs.dtype == F32 else 1)
        self.P.add('tensor', lambda e: e.matmul(out, lhsT, rhs, start=start, stop=stop),
                   reads=[lhsT, rhs], writes=[out], cost=c_)

    def tr(self, out, in_, ident):
        self.P.add('tensor', lambda e: e.transpose(out, in_, ident), reads=[in_, ident], writes=[out], cost=0.1)

    def act(self, out, in_, func, bias=0.0, scale=1.0, accum=None):
        kw = {}
        if accum is not None:
            kw['accum_out'] = accum
        self.P.add('scalar', lambda e: e.activation(out, in_, func, bias=bias, scale=scale, **kw),
                   reads=self._aps([in_, bias, scale]), writes=self._aps([out, accum]), cost=0.2 + _fs(out) * 0.0009)

    def ts(self, eng, out, in0, s1, s2, op0, op1=None, accum=None):
        kw = {}
        if accum is not None:
            kw['accum_out'] = accum
        if op1 is None:
            op1 = ALU.bypass
        self.P.add(eng, lambda e: e.tensor_scalar(out, in0, s1, s2, op0, op1, **kw),
                   reads=self._aps([in0, s1, s2]), writes=self._aps([out, accum]),
                   cost=(0.15 + _fs(out) * 0.0011) * (2 if eng == 'gpsimd' else 1))

    def tt(self, eng, out, in0, in1, op):
        self.P.add(eng, lambda e: e.tensor_tensor(out, in0, in1, op), reads=[in0, in1], writes=[out],
                   cost=(0.15 + _fs(out) * 0.0011) * (2 if eng == 'gpsimd' else 1))

    def stt(self, out, in0, scalar, in1, op0, op1):
        self.P.add('vector', lambda e: e.scalar_tensor_tensor(out, in0, scalar, in1, op0, op1),
                   reads=self._aps([in0, scalar, in1]), writes=[out], cost=0.15 + _fs(out) * 0.0011)

    def copy(self, eng, out, in_):
        if eng == 'scalar':
            self.P.add('scalar', lambda e: e.copy(out, in_), reads=[in_], writes=[out], cost=0.2 + _fs(out) * 0.0009)
        else:
            self.P.add(eng, lambda e: e.tensor_copy(out, in_), reads=[in_], writes=[out],
                       cost=(0.1 + _fs(out) * 0.0006) * (2 if eng == 'gpsimd' else 1))

    def memset(self, eng, ap, v):
        self.P.add(eng, lambda e: e.memset(ap, v), writes=[ap])

    def recip(self, out, in_):
        self.P.add('vector', lambda e: e.reciprocal(out, in_), reads=[in_], writes=[out])

    def bn_stats(self, out, in_):
        self.P.add('vector', lambda e: e.bn_stats(out, in_), reads=[in_], writes=[out])

    def bn_aggr(self, out, in_):
        self.P.add('vector', lambda e: e.bn_aggr(out, in_), reads=[in_], writes=[out])


class _Stop(Exception):
    pass


def build(L, dbg=(), limit=99):
    NT = L // 128
    NST = L // 512
    nc = bass.Bass("TRN2", target_bir_lowering=False)

    def din(name, shape):
        return nc.dram_tensor(name, list(shape), F32, kind="ExternalInput").ap()

    x_d = din("x", [L, 1024])
    wd = {k: din(k, [v[0] * 128, v[1] * v[2]]) for k, v in W_SHAPES.items()}
    wbf = {k: nc.dram_tensor(k + "_bf", [v[0] * 128, v[1] * v[2]], BF16, kind="Internal").ap()
           for k, v in W_SHAPES.items()}
    sd = {k: din(k, v) for k, v in S_SHAPES.items()}
    cd = {k: din(k, v) for k, v in C_SHAPES.items()}
    out_d = nc.dram_tensor("out", [L, 1024], F32, kind="ExternalOutput").ap()
    zd_t = nc.dram_tensor("zscr", [2, 128, 4096], F32, kind="Internal")
    zd = zd_t.ap()
    dbg_d = {}

    es = ExitStack()
    with es:
        def T(name, shape, dt):
            return es.enter_context(nc.sbuf_tensor("sb_" + name, list(shape), dt))
        P = Prog(nc, es)
        k = K(P)
        V, G, A = 'vector', 'gpsimd', 'scalar'

        identb = T("identb", [128, 128], BF16)
        Tm = T("Tm", [128, 128], F32)
        U = T("U", [128, 128], F32)
        onesblk = T("onesblk", [128, 256], F32)
        cmneg = T("cmneg", [128, 128], F32)
        lnp = T("lnp", [128, 2, 1024], F32)
        cw = T("cw", [128, 24, 4], F32)
        cb = T("cb", [128, 24], F32)
        fcw = T("fcw", [128, 44, 3], F32)
        fcb = T("fcb", [128, 44], F32)
        hpb = T("hpb", [128, 3, 32], F32)
        Aneg = T("Aneg", [128, 32], F32)
        ng = T("ng", [128, 16], F32)
        kvg = T("kvg", [128, 256], F32)
        ikp = T("ikp", [128, 2, 64], F32)
        wukT = T("wukT", [128, 8, 256], BF16)
        wuv = T("wuv", [128, 2, 1024], BF16)
        biasT = T("biasT", [128, 16, 256], BF16)
        kvnT = T("kvnT", [128, 2, L], BF16)
        kvtok = T("kvtok", [128, NT, 258], BF16)
        kidxT = T("kidxT", [128, L], BF16)
        S = T("S", [128, 4, 512], F32)
        hal = T("hal", [128, 24, 3], F32)
        fhal = T("fhal", [128, 44, 2], F32)
        xt = T("xt", [128, 1024], F32)
        hb = T("hb", [128, 1024], BF16)
        hT = T("hT", [128, 8, 512], BF16)
        wb = [T("wb0", [128, 4096], BF16), T("wb1", [128, 4096], BF16)]
        s16 = T("s16", [128, 4096], F32)
        ybuf = T("ybuf", [128, 24, 512], BF16)
        sA = T("sA", [128, 4096], BF16)
        sB = T("sB", [128, 4096], BF16)
        sC = T("sC", [128, 4096], BF16)
        sD = T("sD", [128, 4096], BF16)
        fs = [T("fs%d" % i, [128, 516], F32) for i in range(4)]
        junk = T("junk", [128, max(L, 4096)], U8)
        st = T("st", [128, 64], F32)
        dts = T("dts", [128, 4, 32], F32)
        wi = T("wi", [128, 4, 8], F32)
        qTm = xt[:].bitcast(BF16).rearrange("p (a k t) -> p a k t", a=2, k=8)
        qiTm = lnp[:].rearrange("p a b -> p (a b)").bitcast(BF16)[:, 0:1024].rearrange("p (a k t) -> p a k t", a=2, k=4)
        maskT2 = T("maskT2", [128, 4096], BF16)
        _m2f = maskT2[:].bitcast(F32)
        fs.append(_m2f[:, 0:516])
        fs.append(_m2f[:, 516:1032])
        sst = [T("sst%d" % i, [128, 64], F32) for i in range(2)]
        cbms = [T("cbm%d" % i, [128, 128], F32) for i in range(2)]
        pmask = T("pmask", [128, 2], F32)
        pw2 = T("pw2", [128, 64], F32)
        bst = T("bst", [128, 64], F32)
        ps = [es.enter_context(nc.psum_tensor("ps%d" % i, [128, 512], F32)) for i in range(8)]
        psn = [0]

        psmod = [8]

        def nps():
            psn[0] = (psn[0] + 1) % psmod[0]
            return ps[psn[0]]

        def psb(p):
            return p[:].bitcast(BF16)

        def dump(name, ap, shape, dt=F32):
            if name in dbg:
                d = nc.dram_tensor("dbg_" + name, list(shape), dt, kind="ExternalOutput").ap()
                P.dma('sync', d, ap)

        wbn = [0]

        def load_w(name, blk, eng='sync'):
            nb, kc, n = W_SHAPES[name]
            wbn[0] ^= 1
            t = wb[wbn[0]]
            P.dma(eng, t[:, 0:kc * n], wbf[name][blk * 128:(blk + 1) * 128, :])
            return t[:, 0:kc * n].rearrange("p (k n) -> p k n", n=n)

        for t_, nm in ((Tm, 'Tm'), (U, 'U'), (onesblk, 'onesblk'), (cmneg, 'cmneg'), (cw, 'cw'), (cb, 'cb'),
                       (fcw, 'fcw'), (fcb, 'fcb'), (ng, 'ng')):
            P.dma('sync', t_[:], cd[nm] if nm in cd else sd[nm])
        P.dma('sync', hpb[:], sd['hp'].partition_broadcast(128)[:, 0])
        P.dma('sync', kvg[:], sd['kvg'].partition_broadcast(128)[:, 0])
        P.dma('sync', ikp[:], sd['ikp'].partition_broadcast(128)[:, 0])
        P.dma('sync', s16[:, 0:128], cd['identf'])
        k.copy(V, identb[:], s16[:, 0:128])
        k.act(Aneg[:], hpb[:, 1, :], AF.Exp)
        k.ts(V, Aneg[:], Aneg[:], -1.0, None, ALU.mult)
        P.dma('sync', s16[:, 0:2048], sd['wukT'].rearrange("p a b -> p (a b)"))
        k.copy(V, wukT[:].rearrange("p a b -> p (a b)"), s16[:, 0:2048])
        P.dma('sync', s16[:, 2048:4096], sd['wuv'].rearrange("p a b -> p (a b)"))
        k.copy(G, wuv[:].rearrange("p a b -> p (a b)"), s16[:, 2048:4096])
        k.memset(V, hal[:], 0.0)
        k.memset(V, fhal[:], 0.0)
        k.memset(V, S[:], 0.0)
        k.memset(G, kvtok[:, :, 256:258], 1.0)
        k.memset(V, pmask[:], 0.0)
        k.memset(V, pmask[0:64, 0:1], 1.0)
        k.memset(V, pmask[64:128, 1:2], 1.0)
        for it in range(NBIS + 1):
            k.memset(G, pw2[:, it:it + 1], 2.0 ** (-it))
        rb = fs[0][0:32, 0:16]
        P.dma('sync', rb, sd['relb'])
        oh = fs[1][0:32, 0:512].rearrange("p (a b) -> p a b", a=2)
        P.dma('sync', oh, cd['ohb'].rearrange("a p b -> p a b"))
        o32 = fs[2][0:32, 0:128]
        P.dma('sync', o32, cd['ones32'])
        ybf = ybuf[:].rearrange("p a b -> p (a b)").bitcast(F32)
        for kk in range(2):
            rhsb = ybf[0:32, 0:4096].rearrange("p (h j) -> p h j", h=16)
            for h in range(16):
                k.ts(V, rhsb[:, h, :], oh[:, kk, :], rb[:, h:h + 1], None, ALU.mult)
            zs = s16[:, 0:4096]
            for b in range(8):
                p_ = nps()
                k.mm(p_[:], o32, ybf[0:32, b * 512:(b + 1) * 512])
                k.copy(V, zs[:, b * 512:(b + 1) * 512], p_[:])
            P.dma('sync', zd[kk], zs)
            src = bass.AP(tensor=zd_t, offset=kk * 128 * 4096 + 127, ap=[[4095, 128], [256, 16], [1, 128]])
            stg = ybf[:, 4096:6144].rearrange("p (h t) -> p h t", h=16)
            P.dma('sync', stg, src, reads=['zscr'], writes=[ybuf])
            k.ts(V, biasT[:, :, kk * 128:(kk + 1) * 128], stg, 8.0, None, ALU.mult)
        ci = 0
        for name, (nb, kc, n) in W_SHAPES.items():
            for b in range(nb):
                stg = s16[:, 0:kc * n] if ci % 2 == 0 else ybf[:, 0:kc * n]
                P.dma('sync' if (ci % 2 == 0 or OPT.get('nosq')) else 'scalar', stg, wd[name][b * 128:(b + 1) * 128, :])
                wbn[0] ^= 1
                o = wb[wbn[0]][:, 0:kc * n]
                k.copy([V, G, A][ci % 3], o, stg)
                P.dma('gpsimd', wbf[name][b * 128:(b + 1) * 128, :], o)
                ci += 1

        def rstd_from(var_ap, out_ap, scale, eps):
            k.ts(V, out_ap, var_ap, scale, eps, ALU.mult, ALU.add)
            k.act(out_ap, out_ap, AF.Ln)
            k.act(out_ap, out_ap, AF.Exp, scale=-0.5)

        def layernorm(src, dst, dstb=None):
            k.bn_stats(st[:, 0:6], src[:, 0:512])
            k.bn_stats(st[:, 6:12], src[:, 512:1024])
            k.bn_aggr(st[:, 12:14], st[:, 0:12].rearrange("p (a b) -> p a b", a=2))
            rstd_from(st[:, 13:14], st[:, 14:15], 1.0, EPS)
            k.ts(V, src, src, st[:, 12:13], st[:, 14:15], ALU.subtract, ALU.mult)
            k.tt(V, src, src, lnp[:, 0, :], ALU.mult)
            if dst is not None:
                k.tt(V, dst, src, lnp[:, 1, :], ALU.add)
                if dstb is not None:
                    k.copy(A, dstb, dst)
            else:
                k.tt(V, dstb, src, lnp[:, 1, :], ALU.add)

        def to_hT(i):
            p_ = nps()
            pb = psb(p_)
            for kc in range(8):
                k.tr(pb[:, kc * 128:(kc + 1) * 128], hb[:, kc * 128:(kc + 1) * 128], identb[:])
            k.copy(A, hT[:, :, i * 128:(i + 1) * 128], pb[:, 0:1024].rearrange("p (k t) -> p k t", k=8))

        def load_ln(name):
            P.dma('sync', lnp[:].rearrange("p a b -> p (a b)"),
                  sd[name].rearrange("o a b -> o (a b)").partition_broadcast(128)[:, 0])

        cvn = [0]

        def conv(p_in, cidx, out_ap, halo, wts, bias, ntap, func):
            h = ntap - 1
            cvn[0] ^= 1
            cbuf = fs[0][:, :] if cvn[0] else fs[4]
            acc = fs[1][:, :] if cvn[0] else fs[5]
            k.copy(V, cbuf[:, 0:h], halo[:, cidx, :])
            k.copy(A, cbuf[:, h:h + 512], p_in)
            if OPT.get('noA'):
                k.ts(V, acc[:, 0:512], cbuf[:, h:h + 512], wts[:, cidx, h:h + 1], bias[:, cidx:cidx + 1], ALU.mult, ALU.add)
            else:
                k.act(acc[:, 0:512], p_in, AF.Identity, bias=bias[:, cidx:cidx + 1], scale=wts[:, cidx, h:h + 1])
            k.copy(G, halo[:, cidx, :], cbuf[:, 512:512 + h])
            for tp in range(0, h):
                last = (tp == h - 1) and func is None
                k.stt(out_ap if last else acc[:, 0:512], cbuf[:, tp:tp + 512], wts[:, cidx, tp:tp + 1],
                      acc[:, 0:512], ALU.mult, ALU.add)
            if func is not None:
                k.act(out_ap, acc[:, 0:512], func)

        def fm_proj(wv, m, p_):
            for kc in range(8):
                k.mm(p_[:], wv[:, kc, m * 128:(m + 1) * 128], hT[:, kc, :], start=(kc == 0), stop=(kc == 7))

        szg = sA[:].bitcast(F32).rearrange("p (i c) -> p i c", i=4)
        qT = sA[:].rearrange("p (k t) -> p k t", k=8)
        sgs = sA[:, 0:2048].rearrange("p (k t) -> p k t", k=4)
        sga = sA[:, 2048:4096].rearrange("p (k t) -> p k t", k=4)
        sBf = sB[:].bitcast(F32)
        rhsD = sBf[:, 0:1024]
        expD = sBf[:, 1024:2048]
        maskT = sB[:].rearrange("p (k t) -> p k t", t=128)
        Xg = sC[:, 0:2048].rearrange("p (i c) -> p i c", i=4)
        Bg = sC[:, 2048:2560].rearrange("p (i c) -> p i c", i=4)
        BT = sC[:, 2560:3072]
        CT = sC[:, 3072:3584]
        qlT = sC[:].rearrange("p (c h t) -> p c h t", c=2, h=16)
        mT = sC[:].rearrange("p (k t) -> p k t", k=8)
        Gm = sD[:, 0:1024]
        Xdt = sD[:, 1024:1536]
        Xd = sD[:, 1536:2048]
        S0b = sD[:, 2048:2560]
        S1b = sD[:, 2560:3072]
        vn = sD[:, 3072:3584]
        qiT = sD[:, 0:2048].rearrange("p (k t) -> p k t", k=4)
        PTs = [sD[:, 2048:2560], sD[:, 2560:3072]]
        mblk = sD[:, 3072:3584]
        ol = sD[:, 3584:3840]
        oT = sD[:, 3840:4096].rearrange("p (c t) -> p c t", c=2)
        ysT = ybuf[:, 0:16, :]
        yaT = ybuf[:, 16:24, :]
        gT = ybuf
        score = s16
        h1 = s16[:].rearrange("p (i c) -> p i c", i=4)

        def bc(ap, n):
            return ap.unsqueeze(2).to_broadcast([128, ap.shape[1], n])

        def chk(n):
            if limit <= n:
                raise _Stop()

        try:
          chk(1)
          for sti in range(NST):
              load_ln('ln0')
              for i in range(4):
                  Tg = sti * 4 + i
                  P.dma('sync', xt[:], x_d[Tg * 128:(Tg + 1) * 128, :])
                  layernorm(xt[:], None, hb[:])
                  to_hT(i)
              if sti == 0:
                  dump("hT", hT[:], [128, 8, 512], BF16)
              chk(2)
              wv = load_w('wsm', 0)
              for i in range(4):
                  Tg = sti * 4 + i
                  p_ = nps()
                  for kc in range(8):
                      k.mm(p_[:, 0:360], hT[:, kc, i * 128:(i + 1) * 128], wv[:, kc, :], start=(kc == 0), stop=(kc == 7))
                  k.tt(V, st[:, 16:48], p_[:, 0:32], hpb[:, 0, :], ALU.add)
                  k.act(st[:, 16:48], st[:, 16:48], AF.Exp)
                  k.act(dts[:, i, :], st[:, 16:48], AF.Ln, bias=1.0)
                  k.act(fs[2][:, 0:256], p_[:, 32:288], AF.Square, accum=st[:, 48:49])
                  rstd_from(st[:, 48:49], st[:, 49:50], 1.0 / 256, EPS)
                  k.ts(V, fs[2][:, 0:256], p_[:, 32:288], st[:, 49:50], None, ALU.mult)
                  k.tt(V, kvtok[:, Tg, 0:256], fs[2][:, 0:256], kvg[:], ALU.mult)
                  p2 = nps()
                  pb = psb(p2)
                  for cc in range(2):
                      k.tr(pb[:, cc * 128:(cc + 1) * 128], kvtok[:, Tg, cc * 128:(cc + 1) * 128], identb[:])
                  k.copy(A, kvnT[:, :, Tg * 128:(Tg + 1) * 128], pb[:, 0:256].rearrange("p (c t) -> p c t", c=2))
                  k.bn_stats(st[:, 50:56], p_[:, 288:352])
                  k.bn_aggr(st[:, 56:58], st[:, 50:56])
                  rstd_from(st[:, 57:58], st[:, 58:59], 1.0, EPS)
                  k.ts(V, fs[3][:, 0:64], p_[:, 288:352], st[:, 56:57], st[:, 58:59], ALU.subtract, ALU.mult)
                  k.tt(V, fs[3][:, 0:64], fs[3][:, 0:64], ikp[:, 0, :], ALU.mult)
                  k.tt(V, hb[:, 0:64], fs[3][:, 0:64], ikp[:, 1, :], ALU.add)
                  k.copy(V, hb[:, 64:128], hb[:, 0:64])
                  p3 = nps()
                  pb3 = psb(p3)
                  k.tr(pb3[:, 0:128], hb[:, 0:128], identb[:])
                  k.copy(A, kidxT[:, Tg * 128:(Tg + 1) * 128], pb3[:, 0:128])
                  k.ts(V, wi[:, i, :], p_[:, 352:360], (8 ** -0.5) * (64 ** -0.5), None, ALU.mult)
              if sti == 0:
                  dump("dts", dts[:], [128, 4, 32])
                  dump("kvtok", kvtok[:, 0:4, :], [128, 4, 258], BF16)
                  dump("kidxT", kidxT[:, 0:512], [128, 512], BF16)
              chk(3)
              sAb = sA[:]
              gsets = []
              for q_ in range(2):
                  base = sC[:] if q_ == 0 else ybuf[:, 16:24, :].rearrange("p a b -> p (a b)")
                  gsets.append(dict(
                      Xg=base[:, 0:2048].rearrange("p (i c) -> p i c", i=4),
                      Bg=base[:, 2048:2560].rearrange("p (i c) -> p i c", i=4),
                      BT=base[:, 2560:3072], CT=base[:, 3072:3584],
                      xsT=base[:, 3584:4096],
                      sz=sAb[:, q_ * 2048:(q_ + 1) * 2048].rearrange("p (i c) -> p i c", i=4)))

              pjb = [0]

              def pnps():
                  pjb[0] ^= 1
                  return ps[pjb[0]]

              def proj_gen(g):
                  gs = gsets[g % 2]
                  wv = load_w('wxs', g)
                  for j in range(4):
                      p_ = pnps()
                      fm_proj(wv, j, p_)
                      yield
                      conv(p_[:], 4 * g + j, gs['xsT'], hal, cw, cb, 4, AF.Silu)
                      yield
                      p2 = pnps()
                      pb = psb(p2)
                      for i in range(4):
                          k.tr(pb[:, i * 128:(i + 1) * 128], gs['xsT'][:, i * 128:(i + 1) * 128], identb[:])
                      k.copy(A, gs['Xg'][:, :, j * 128:(j + 1) * 128], pb[:, 0:512].rearrange("p (i c) -> p i c", i=4))
                      yield
                  wv = load_w('wbc', g)
                  p_ = pnps()
                  fm_proj(wv, 0, p_)
                  yield
                  conv(p_[:], 16 + g, gs['BT'], hal, cw, cb, 4, AF.Silu)
                  yield
                  p2 = pnps()
                  pb = psb(p2)
                  for i in range(4):
                      k.tr(pb[:, i * 128:(i + 1) * 128], gs['BT'][:, i * 128:(i + 1) * 128], identb[:])
                  k.copy(A, gs['Bg'], pb[:, 0:512].rearrange("p (i c) -> p i c", i=4))
                  yield
                  p_ = pnps()
                  fm_proj(wv, 1, p_)
                  yield
                  conv(p_[:], 20 + g, gs['CT'], hal, cw, cb, 4, AF.Silu)
                  yield
                  wv = load_w('wz', g)
                  for i in range(4):
                      p_ = pnps()
                      for kc in range(8):
                          k.mm(p_[:], hT[:, kc, i * 128:(i + 1) * 128], wv[:, kc, :], start=(kc == 0), stop=(kc == 7))
                      k.act(gs['sz'][:, i, :], p_[:], AF.Silu)
                      yield

              def iter_gen(g, i):
                  gs = gsets[g % 2]
                  Xg_, Bg_, BT_, CT_, szg_ = gs['Xg'], gs['Bg'], gs['BT'], gs['CT'], gs['sz']
                  g8 = slice(8 * g, 8 * g + 8)
                  tc_ = slice(i * 128, (i + 1) * 128)
                  par = i % 2
                  ba, bb_, bc_ = (ps[2], ps[3], ps[4]) if par == 0 else (ps[5], ps[6], ps[7])
                  sv = sst[par]
                  if par == 0:
                      rhsD_, expD_, Gm_, Xdt_, Xd0_, Xd1_ = rhsD, expD, Gm, Xdt, Xd, sD[:, 3584:4096]
                      S0b_, S1b_, vn_, t1, yv = S0b, S1b, vn, fs[2][:, 0:512], fs[3][:, 0:512]
                  else:
                      s16b = s16[:].bitcast(BF16)
                      jb = junk[:].bitcast(BF16)
                      rhsD_, expD_ = s16[:, 0:1024], s16[:, 1024:2048]
                      t1, yv = s16[:, 2048:2560], s16[:, 2560:3072]
                      Gm_, Xdt_, Xd0_ = s16b[:, 6144:7168], s16b[:, 7168:7680], s16b[:, 7680:8192]
                      Xd1_, S0b_, S1b_, vn_ = jb[:, 0:512], jb[:, 512:1024], jb[:, 1024:1536], jb[:, 1536:2048]
                  cbm = cbms[par][:]
                  a8 = sv[:, 0:8]
                  k.tt(V, a8, dts[:, i, g8], Aneg[:, g8], ALU.mult)
                  pE = ba
                  k.mm(pE[:, 0:8], Tm[:], a8)
                  k.mm(pE[:, 8:16], U[:], a8)
                  k.mm(pE[:, 16:24], onesblk[:, 0:128], a8)
                  k.mm(pE[:, 24:32], onesblk[:, 128:256], a8)
                  EX = sv[:, 8:40]
                  k.act(EX, pE[:, 0:32], AF.Exp)
                  yield
                  k.ts(V, sv[:, 40:48], EX[:, 8:16], pmask[:, 0:1], None, ALU.mult)
                  k.ts(V, sv[:, 48:56], EX[:, 8:16], pmask[:, 1:2], None, ALU.mult)
                  rD3 = rhsD_.rearrange("p (h l) -> p h l", h=8)
                  k.tt(V, rD3, bc(a8, 128), Tm[:].unsqueeze(1).to_broadcast([128, 8, 128]), ALU.mult)
                  yield
                  for b in range(2):
                      pD = bb_ if b == 0 else bc_
                      k.mm(pD[:], U[:], rhsD_[:, b * 512:(b + 1) * 512])
                      k.act(expD_[:, b * 512:(b + 1) * 512], pD[:], AF.Exp)
                  yield
                  pC = ba
                  k.mm(pC[:, 0:128], BT_[:, tc_], CT_[:, tc_])
                  k.tt(V, cbm, pC[:, 0:128], Tm[:], ALU.mult)
                  yield
                  k.tt(V, Gm_.rearrange("p (h l) -> p h l", h=8), expD_.rearrange("p (h l) -> p h l", h=8),
                       cbm.unsqueeze(1).to_broadcast([128, 8, 128]), ALU.mult)
                  X3 = Xg_[:, i, :].rearrange("p (h q) -> p h q", h=8)
                  Xdt3 = Xdt_.rearrange("p (h q) -> p h q", h=8)
                  k.tt(G, Xdt3, X3, bc(dts[:, i, g8], 64), ALU.mult)
                  yield
                  k.tt(G, Xd0_.rearrange("p (h q) -> p h q", h=8), Xdt3, bc(sv[:, 40:48], 64), ALU.mult)
                  k.tt(G, Xd1_.rearrange("p (h q) -> p h q", h=8), Xdt3, bc(sv[:, 48:56], 64), ALU.mult)
                  yield 'B'
                  pY = ba
                  for h in range(8):
                      k.mm(pY[:, h * 64:(h + 1) * 64], Gm_[:, h * 128:(h + 1) * 128], Xdt_[:, h * 64:(h + 1) * 64])
                  pS0 = bb_
                  pS1 = bc_
                  k.mm(pS0[:], Bg_[:, i, :], Xd0_)
                  k.mm(pS1[:], Bg_[:, i, :], Xd1_)
                  yield
                  Sg = S[:, g, :]
                  Sg3 = Sg.rearrange("p (h q) -> p h q", h=8)
                  k.copy(A, S0b_, Sg)
                  k.tt(V if True else G, Sg3, Sg3, bc(EX[:, 16:24], 64), ALU.mult)
                  k.tt(V, Sg, Sg, pS0[:], ALU.add)
                  yield
                  k.copy(A, S1b_, Sg)
                  k.tt(V if True else G, Sg3, Sg3, bc(EX[:, 24:32], 64), ALU.mult)
                  k.tt(V, Sg, Sg, pS1[:], ALU.add)
                  yield
                  pO = bb_
                  k.mm(pO[0:64, :], CT_[:, i * 128:i * 128 + 64], S0b_)
                  k.mm(pO[64:128, :], CT_[:, i * 128 + 64:(i + 1) * 128], S1b_)
                  k.tt(V, t1.rearrange("p (h q) -> p h q", h=8), pO[:].rearrange("p (h q) -> p h q", h=8),
                       bc(EX[:, 0:8], 64), ALU.mult)
                  yield
                  k.tt(V, yv, t1, pY[:], ALU.add)
                  k.tt(G, t1.rearrange("p (h q) -> p h q", h=8), X3, bc(hpb[:, 2, g8], 64), ALU.mult)
                  yield
                  k.tt(V if True else G, yv, yv, t1, ALU.add)
                  k.tt(V if True else G, yv, yv, szg_[:, i, :], ALU.mult)
                  k.act(t1, yv, AF.Square, accum=sv[:, 56:57])
                  yield
                  rstd_from(sv[:, 56:57], sv[:, 57:58], 1.0 / 512, EPS)
                  k.ts(V, vn_, yv, sv[:, 57:58], None, ALU.mult)
                  yield
                  p2 = bc_
                  pb = psb(p2)
                  for j in range(4):
                      k.tr(pb[:, j * 128:(j + 1) * 128], vn_[:, j * 128:(j + 1) * 128], identb[:])
                  for j in range(4):
                      k.ts(V, ysT[:, 4 * g + j, tc_], pb[:, j * 128:(j + 1) * 128],
                           ng[:, 4 * g + j:4 * g + j + 1], None, ALU.mult)
                  yield

              def run_all(gen):
                  for _ in gen:
                      pass

              def run_until_B(gen):
                  for r in gen:
                      if r == 'B':
                          return

              def interleave(gens):
                  gens = [g_ for g_ in gens if g_ is not None]
                  while gens:
                      alive = []
                      for g_ in gens:
                          try:
                              r = next(g_)
                              alive.append(g_)
                          except StopIteration:
                              pass
                      gens = alive

              def take(gen, n):
                  def sub():
                      for _ in range(n):
                          try:
                              next(gen)
                          except StopIteration:
                              return
                          yield
                  return sub()

              def untilB(gen):
                  def sub():
                      for r in gen:
                          if r == 'B':
                              return
                          yield
                  return sub()

              run_all(proj_gen(0))
              for g in range(4):
                  its = [iter_gen(g, i) for i in range(4)]
                  pj = proj_gen(g + 1) if g < 3 else None
                  run_until_B(its[0])
                  for n in range(4):
                      nxt = untilB(its[n + 1]) if n < 3 else None
                      pjs = take(pj, 6) if pj is not None else None
                      interleave([its[n], nxt, pjs])
                  if pj is not None:
                      run_all(pj)
                  if sti == 0 and g == 0:
                      dump("Xg", gsets[0]['Xg'], [128, 4, 512], BF16)
                      dump("CT", gsets[0]['CT'], [128, 512], BF16)
              if sti == 0:
                  dump("ysT", ysT, [128, 16, 512], BF16)
              chk(4)
              for b in range(2):
                  wv = load_w('wq', b)
                  for m in range(4):
                      p_ = nps()
                      fm_proj(wv, m, p_)
                      k.copy(A, qT[:, b * 4 + m, :], p_[:])
              wv = load_w('wqi', 0)
              for m in range(4):
                  p_ = nps()
                  fm_proj(wv, m, p_)
                  k.copy(A, qiT[:, m, :], p_[:])
              chk(4.1)
              psmod[0] = 4
              k.memset(V, qTm[64:128, 0, :, :], 0.0)
              k.memset(G, qTm[0:64, 1, :, :], 0.0)
              k.memset(V, qiTm[64:128, 0, :, :], 0.0)
              k.memset(G, qiTm[0:64, 1, :, :], 0.0)
              maskTs = [maskT, maskT2[:].rearrange("p (k t) -> p k t", t=128)]

              def idx_a(i):
                  Tg = sti * 4 + i
                  nk = (Tg + 1) * 128
                  tc_ = slice(i * 128, (i + 1) * 128)
                  nb4 = (nk + 511) // 512
                  k.copy(V, qiTm[0:64, 0, :, :], qiT[0:64, :, tc_])
                  k.copy(G, qiTm[64:128, 1, :, :], qiT[64:128, :, tc_])
                  for kb4 in range(nb4):
                      c0 = kb4 * 512
                      ncol = min(512, nk - c0)
                      for h in range(8):
                          p_ = nps()
                          k.mm(p_[:, 0:ncol], qiTm[:, h % 2, h // 2, :], kidxT[:, c0:c0 + ncol])
                          rl = fs[h % 2][:, 0:ncol]
                          k.act(rl, p_[:, 0:ncol], AF.Relu)
                          if h == 0:
                              k.ts(V, score[:, c0:c0 + ncol], rl, wi[:, i, 0:1], None, ALU.mult)
                          else:
                              k.stt(score[:, c0:c0 + ncol], rl, wi[:, i, h:h + 1], score[:, c0:c0 + ncol],
                                    ALU.mult, ALU.add)
                  thr = bst[:, 2:3]
                  if nk > 256:
                      Bm = bst[:, 4:5]
                      P.add(V, lambda e: e.tensor_reduce(Bm, score[:, 0:nk], AX.X, ALU.max, apply_absolute_value=True),
                            reads=[score[:, 0:nk]], writes=[Bm])
                      k.ts(V, Bm, Bm, 1e-20, 1.0000001, ALU.add, ALU.mult)
                      hwt = bst[:, 8:8 + NBIS + 1]
                      k.ts(V, hwt, pw2[:, 0:NBIS + 1], Bm, None, ALU.mult)
                      hwn = bst[:, 32:32 + NBIS + 1]
                      k.ts(V, hwn, hwt, -0.5, None, ALU.mult)
                  k.tt(V, score[:, Tg * 128:(Tg + 1) * 128], score[:, Tg * 128:(Tg + 1) * 128], cmneg[:], ALU.add)
                  if nk > 256:
                      k.memset(V, bst[:, 0:1], 0.0)
                  else:
                      k.memset(V, thr, -10000.0)

              def bis_part(i, it0, it1):
                  Tg = sti * 4 + i
                  nk = (Tg + 1) * 128
                  if nk <= 256:
                      return
                  mid = bst[:, 0:1]
                  cnt = bst[:, 1:2]
                  dd = bst[:, 3:4]
                  hwt = bst[:, 8:8 + NBIS + 1]
                  hwn = bst[:, 32:32 + NBIS + 1]
                  for it in range(it0, it1):
                      k.ts(V, junk[:, 0:nk], score[:, 0:nk], mid, None, ALU.is_ge, ALU.add, accum=cnt)
                      k.ts(V, dd, cnt, 255.5, hwt[:, it:it + 1], ALU.is_ge, ALU.mult)
                      k.stt(mid, dd, hwn[:, it:it + 1], mid, ALU.add, ALU.add)
                  if it1 == NBIS:
                      k.tt(V, bst[:, 2:3], mid, hwt[:, NBIS:NBIS + 1], ALU.subtract)

              def idx_b(i):
                  Tg = sti * 4 + i
                  nk = (Tg + 1) * 128
                  nb4 = (nk + 511) // 512
                  mT_ = maskTs[i % 2]
                  thr = bst[:, 2:3]
                  for kb4 in range(nb4):
                      c0 = kb4 * 512
                      ncol = min(512, nk - c0)
                      mb_ = mblk if kb4 % 2 == 0 else sD[:, 3584:4096]
                      k.ts(V, mb_[:, 0:ncol], score[:, c0:c0 + ncol], thr, None, ALU.is_ge)
                      p2 = nps()
                      pb = psb(p2)
                      for q in range(ncol // 128):
                          k.tr(pb[:, q * 128:(q + 1) * 128], mb_[:, q * 128:(q + 1) * 128], identb[:])
                      k.copy(A, mT_[:, kb4 * 4:kb4 * 4 + ncol // 128, :],
                             pb[:, 0:ncol].rearrange("p (q t) -> p q t", t=128))

              bias8 = biasT
              PT3s = [sD[:, 2048:2560], sD[:, 2560:3072], sD[:, 3072:3584]]
              Lb = [ps[1], ps[2], ps[3]]
              olb = hb[:, 0:256]
              oTb = hb[:, 256:512].rearrange("p (c t) -> p c t", c=2)

              def main_stage(i):
                  Tg = sti * 4 + i
                  tc_ = slice(i * 128, (i + 1) * 128)
                  mT_ = maskTs[i % 2]
                  k.copy(V, qTm[0:64, 0, :, :], qT[0:64, :, tc_])
                  k.copy(G, qTm[64:128, 1, :, :], qT[64:128, :, tc_])
                  for hg in range(4):
                      for cc in range(2):
                          p_ = ps[0]
                          for hh in range(4):
                              h = 4 * hg + hh
                              k.mm(p_[:, hh * 128:(hh + 1) * 128], wukT[:, h // 2, cc * 128:(cc + 1) * 128],
                                   qTm[:, h % 2, h // 2, :])
                          k.copy(A, qlT[:, cc, 4 * hg:4 * hg + 4, :], p_[:].rearrange("p (h t) -> p h t", h=4))
                  tot = 4 * (Tg + 1)
                  done = [0, 0]

                  def tick():
                      done[0] += 1
                      if i < 3:
                          want = min(NBIS, (done[0] * NBIS + tot - 1) // tot)
                          if want > done[1]:
                              bis_part(i + 1, done[1], want)
                              done[1] = want

                  for hgx in range(4):
                      its = [(hgx, kb) for kb in range(Tg + 1)]
                      run_its(its, Tg, tc_, mT_, tick)

              def run_its(its, Tg, tc_, mT_, tick):
                  def qk(n):
                      hg, kb = its[n]
                      pL = Lb[n % 3]
                      near = kb >= Tg - 1
                      for cc in range(2):
                          k.mm(pL[:], kvnT[:, cc, kb * 128:(kb + 1) * 128], qlT[:, cc, 4 * hg:4 * hg + 4, :],
                               start=(cc == 0), stop=(cc == 1 and not near))
                      if near:
                          off = 128 if kb == Tg else 0
                          k.mm(pL[:], identb[:], bias8[:, 4 * hg:4 * hg + 4, off:off + 128], start=False, stop=True)

                  def softmax_part(n):
                      hg, kb = its[n]
                      pL = Lb[n % 3]
                      PT = PT3s[n % 3]
                      k.act(PT, pL[:], AF.Exp, scale=0.125)
                      PT3 = PT.rearrange("p (h t) -> p h t", h=4)
                      k.tt(G if (n % OPT.get('mm', 1000000) != 1) else V, PT3, PT3, mT_[:, kb, :].unsqueeze(1).to_broadcast([128, 4, 128]), ALU.mult)

                  def pv(n):
                      hg, kb = its[n]
                      PT = PT3s[n % 3]
                      for hh in range(4):
                          k.mm(ps[4 + hh][:, 0:257], PT[:, hh * 128:(hh + 1) * 128], kvtok[:, kb, 0:257],
                               start=(kb == 0), stop=(kb == Tg))
                      if kb == Tg:
                          for hh in range(4):
                              h = 4 * hg + hh
                              po = 64 * (h % 2)
                              acc = ps[4 + hh]
                              rv = bst[:, 60 + hh:61 + hh]
                              k.recip(rv, acc[:, 256:257])
                              k.ts(V, olb, acc[:, 0:256], rv, None, ALU.mult)
                              pb = psb(ps[0])
                              for cc in range(2):
                                  k.tr(pb[:, cc * 128:(cc + 1) * 128], olb[:, cc * 128:(cc + 1) * 128], identb[:])
                              k.copy(A, oTb, pb[:, 0:256].rearrange("p (c t) -> p c t", c=2))
                              pYa = ps[0]
                              for cc in range(2):
                                  k.mm(pYa[po:po + 64, 256:384], wuv[:, cc, h * 64:(h + 1) * 64], oTb[:, cc, :],
                                       start=(cc == 0), stop=(cc == 1))
                              if h % 2 == 1:
                                  k.copy(A, yaT[:, h // 2, tc_], pYa[:, 256:384])

                  N = len(its)
                  LA = 2
                  for n in range(min(LA, N)):
                      qk(n)
                  for n in range(N):
                      softmax_part(n)
                      if n + LA < N:
                          qk(n + LA)
                      pv(n)
                      tick()

              idx_a(0)
              bis_part(0, 0, NBIS)
              idx_b(0)
              for i in range(4):
                  if i < 3:
                      idx_a(i + 1)
                  main_stage(i)
                  if i < 3:
                      idx_b(i + 1)
              psmod[0] = 8
              if sti == 0:
                  dump("yaT", yaT, [128, 8, 512], BF16)
              chk(5)
              for q4 in range(2):
                  wv = load_w('wgs', q4)
                  for m in range(4):
                      p_ = nps()
                      fm_proj(wv, m, p_)
                      k.act(sgs[:, m, :], p_[:], AF.Sigmoid)
                  wv = load_w('wga', q4)
                  for m in range(4):
                      p_ = nps()
                      fm_proj(wv, m, p_)
                      k.act(sga[:, m, :], p_[:], AF.Sigmoid)
                  for b2 in range(2):
                      wv = load_w('wbs', 2 * q4 + b2)
                      for m in range(2):
                          p_ = nps()
                          for kc in range(16):
                              k.mm(p_[:], wv[:, kc, m * 128:(m + 1) * 128], ysT[:, kc, :], start=(kc == 0), stop=(kc == 15))
                          k.tt(V, mT[:, 4 * q4 + 2 * b2 + m, :], p_[:], sgs[:, 2 * b2 + m, :], ALU.mult)
                  wv = load_w('wba', q4)
                  for m in range(4):
                      p_ = nps()
                      for kc in range(8):
                          k.mm(p_[:], wv[:, kc, m * 128:(m + 1) * 128], yaT[:, kc, :], start=(kc == 0), stop=(kc == 7))
                      k.tt(V, fs[2][:, 0:512], p_[:], sga[:, m, :], ALU.mult)
                      k.tt(V, mT[:, 4 * q4 + m, :], mT[:, 4 * q4 + m, :], fs[2][:, 0:512], ALU.add)
              if sti == 0:
                  dump("mT", mT, [128, 8, 512], BF16)
              wo = [load_w('wo', 0), load_w('wo', 1)]
              load_ln('ln0')
              for i in range(4):
                  Tg = sti * 4 + i
                  P.dma('sync', xt[:], x_d[Tg * 128:(Tg + 1) * 128, :])
                  layernorm(xt[:], xt[:])
                  for b2 in range(2):
                      p_ = nps()
                      for kc in range(8):
                          k.mm(p_[:], mT[:, kc, i * 128:(i + 1) * 128], wo[b2][:, kc, :], start=(kc == 0), stop=(kc == 7))
                      k.stt(h1[:, i, b2 * 512:(b2 + 1) * 512], xt[:, b2 * 512:(b2 + 1) * 512], ALPHA, p_[:],
                            ALU.mult, ALU.add)
              load_ln('ln1')
              for i in range(4):
                  layernorm(h1[:, i, :], h1[:, i, :], hb[:])
                  to_hT(i)
              if sti == 0:
                  dump("h1", h1, [128, 4, 1024])
              chk(6)
              for blk in range(11):
                  wv = load_w('wup', blk)
                  for jj in range(2):
                      j = 2 * blk + jj
                      p_ = nps()
                      fm_proj(wv, 2 * jj, p_)
                      ga = fs[2][:, 0:512]
                      conv(p_[:], j, ga, fhal, fcw, fcb, 3, AF.Silu)
                      p2 = nps()
                      fm_proj(wv, 2 * jj + 1, p2)
                      gv = fs[3][:, 0:512]
                      conv(p2[:], 22 + j, gv, fhal, fcw, fcb, 3, None)
                      k.tt(V, gT[:, j, :], ga, gv, ALU.mult)
              for nb in range(8):
                  wv = load_w('wdn', nb)
                  p_ = nps()
                  for i in range(4):
                      for j in range(22):
                          k.mm(p_[:, i * 128:(i + 1) * 128], gT[:, j, i * 128:(i + 1) * 128], wv[:, j, :],
                               start=(j == 0), stop=(j == 21))
                  for i in range(4):
                      k.stt(h1[:, i, nb * 128:(nb + 1) * 128], h1[:, i, nb * 128:(nb + 1) * 128], ALPHA,
                            p_[:, i * 128:(i + 1) * 128], ALU.mult, ALU.add)
              load_ln('ln2')
              for i in range(4):
                  Tg = sti * 4 + i
                  layernorm(h1[:, i, :], h1[:, i, :])
                  P.dma('sync', out_d[Tg * 128:(Tg + 1) * 128, :], h1[:, i, :])
        except _Stop:
            pass
        P.emit()
    return nc


def make_inmaps(inputs, L, nb):
    cw_ = host_consts()
    ww = host_weights(inputs)
    ss = host_small(inputs)
    base = {}
    for kk, v in ww.items():
        base[kk] = np.ascontiguousarray(v.reshape(v.shape[0] * 128, -1))
    base.update(ss)
    base.update(cw_)
    x = np.asarray(inputs['x'], np.float32)
    maps = []
    for c in range(nb):
        m = dict(base)
        m['x'] = np.ascontiguousarray(x[c % x.shape[0], :L])
        maps.append(m)
    return maps


def kernel(**inputs):
    L = SEQ
    nc = build(L)
    maps = make_inmaps(inputs, L, 8)
    res = run_bass_kernel_spmd(nc, maps, core_ids=list(range(8)))
    out = np.stack([np.asarray(res.results[c]["out"], np.float32) for c in range(4)], axis=0)
    return out
```
